# Optimizing a Trainium2 kernel written in Bass

```python
import math
import jax
import jax.numpy as jnp
from jax import lax
import numpy as np

D_MODEL = 1024
BATCH = 8
SEQ = 2048
DEPTH = 2

MEM_LEN = 256
HEAD_DIM = 64
SWA_Q_HEADS = 8
SWA_KV_HEADS = 2
SWA_GROUP = SWA_Q_HEADS // SWA_KV_HEADS
WINDOW = 128
SWA_BLOCK = WINDOW
MLSTM_HEADS = 8
MLSTM_HEAD_DIM = 64
MLSTM_CHUNK = 64
CONV_WIDTH = 4
X_HEADS = 4
X_HEAD_DIM = 128
N_BRANCH = 3
D_FF = 4 * D_MODEL
REL_BUCKETS = 32
REL_MAX_EXACT = 16
REL_MAX_DIST = 128
EPS = 1e-6
NEG_INF = -1e30

SWA_Q = SWA_Q_HEADS * HEAD_DIM
SWA_KV = SWA_KV_HEADS * HEAD_DIM
MLSTM_W = MLSTM_HEADS * MLSTM_HEAD_DIM
X_W = X_HEADS * X_HEAD_DIM
IN_SIZES = (SWA_Q, SWA_KV, SWA_KV, MLSTM_W, MLSTM_W, MLSTM_W, MLSTM_HEADS, MLSTM_HEADS, MLSTM_W, X_W, N_BRANCH * D_MODEL)
IN_COLS = sum(IN_SIZES)

kernel_name = "hybrid_swa_mlstm_xattn_gated_block"


def rms_norm(x, g):
    xf = x.astype(jnp.float32)
    y = xf * lax.rsqrt(jnp.mean(xf * xf, axis=-1, keepdims=True) + EPS)
    return (y * g.astype(jnp.float32)).astype(x.dtype)


def t5_causal_bucket(dist):
    n = jnp.maximum(dist, 0)
    nf = jnp.maximum(n, 1).astype(jnp.float32)
    scale = (REL_BUCKETS - REL_MAX_EXACT) / math.log(REL_MAX_DIST / REL_MAX_EXACT)
    large = REL_MAX_EXACT + (jnp.log(nf / REL_MAX_EXACT) * scale).astype(jnp.int32)
    large = jnp.minimum(large, REL_BUCKETS - 1)
    return jnp.where(n < REL_MAX_EXACT, n, large)


def swa_band_bias_and_mask(rel_bias, n_blocks):
    qi = jnp.arange(SWA_BLOCK)[:, None]
    kj = jnp.arange(2 * SWA_BLOCK)[None, :]
    dist = qi + SWA_BLOCK - kj
    band = (dist >= 0) & (dist < WINDOW)
    key_pos = (jnp.arange(n_blocks)[:, None, None] - 1) * SWA_BLOCK + kj[None]
    mask = band[None] & (key_pos >= 0)
    bias = rel_bias.astype(jnp.float32)[t5_causal_bucket(dist)]
    bias = jnp.transpose(bias, (2, 0, 1)).reshape(SWA_KV_HEADS, SWA_GROUP, SWA_BLOCK, 2 * SWA_BLOCK)
    return bias, mask


def swa_attention(q, k, v, sinks, bias, mask):
    B, T, _ = q.shape
    nb = T // SWA_BLOCK
    q = q.reshape(B, nb, SWA_BLOCK, SWA_KV_HEADS, SWA_GROUP, HEAD_DIM)
    k = k.reshape(B, nb, SWA_BLOCK, SWA_KV_HEADS, HEAD_DIM)
    v = v.reshape(B, nb, SWA_BLOCK, SWA_KV_HEADS, HEAD_DIM)

    def with_prev(t):
        prev = jnp.pad(t[:, :-1], ((0, 0), (1, 0), (0, 0), (0, 0), (0, 0)))
        return jnp.concatenate([prev, t], axis=2)

    kb, vb = with_prev(k), with_prev(v)
    s = jnp.einsum("bnqhgd,bnkhd->bnhgqk", q, kb).astype(jnp.float32) * HEAD_DIM ** -0.5
    s = s + bias[None, None]
    s = jnp.where(mask[None, :, None, None], s, NEG_INF)
    sink = sinks.astype(jnp.float32).reshape(SWA_KV_HEADS, SWA_GROUP)[:, :, None, None]
    m = jnp.maximum(jnp.max(s, axis=-1, keepdims=True), sink)
    p = jnp.exp(s - m)
    p = p / (jnp.sum(p, axis=-1, keepdims=True) + jnp.exp(sink - m))
    o = jnp.einsum("bnhgqk,bnkhd->bnqhgd", p.astype(v.dtype), vb)
    return o.reshape(B, T, SWA_Q)


def causal_depthwise_conv(x, w):
    C = x.shape[-1]
    return lax.conv_general_dilated(
        x, w[:, None, :].astype(x.dtype), window_strides=(1,),
        padding=[(CONV_WIDTH - 1, 0)], dimension_numbers=("NWC", "WIO", "NWC"),
        feature_group_count=C)


def mlstm_chunkwise(q, k, v, i_pre, f_pre):
    B, T, H, Dh = q.shape
    L = MLSTM_CHUNK
    nc = T // L
    f32 = jnp.float32
    qc = q.astype(f32).reshape(B, nc, L, H, Dh)
    kc = (k.astype(f32) * Dh ** -0.5).reshape(B, nc, L, H, Dh)
    vc = v.astype(f32).reshape(B, nc, L, H, Dh)
    ig = i_pre.reshape(B, nc, L, H)
    b = jnp.cumsum(jax.nn.log_sigmoid(f_pre).reshape(B, nc, L, H), axis=2)
    g = b[:, :, -1]
    w_log = g[:, :, None] - b + ig
    m_loc = jnp.max(w_log, axis=2)
    wk = jnp.exp(w_log - m_loc[:, :, None])[..., None] * kc
    dC = jnp.einsum("bclhk,bclhv->bchkv", wk, vc)
    dn = jnp.sum(wk, axis=2)

    def step(carry, xs):
        C, n, m = carry
        dC_c, dn_c, g_c, ml_c = xs
        m_new = jnp.maximum(g_c + m, ml_c)
        a = jnp.exp(g_c + m - m_new)
        s = jnp.exp(ml_c - m_new)
        C_new = a[..., None, None] * C + s[..., None, None] * dC_c
        n_new = a[..., None] * n + s[..., None] * dn_c
        return (C_new, n_new, m_new), (C, n, m)

    init = (jnp.zeros((B, H, Dh, Dh), f32), jnp.zeros((B, H, Dh), f32), jnp.zeros((B, H), f32))
    xs = tuple(jnp.moveaxis(t, 1, 0) for t in (dC, dn, g, m_loc))
    _, (C_in, n_in, m_in) = lax.scan(step, init, xs)
    C_in = jnp.moveaxis(C_in, 0, 1)
    n_in = jnp.moveaxis(n_in, 0, 1)
    m_in = jnp.moveaxis(m_in, 0, 1)

    causal = jnp.tril(jnp.ones((L, L), dtype=bool))
    d_log = b[:, :, :, None] - b[:, :, None, :] + ig[:, :, None, :]
    d_log = jnp.where(causal[:, :, None], d_log, NEG_INF)
    inter_log = b + m_in[:, :, None]
    m_t = jnp.maximum(jnp.max(d_log, axis=3), inter_log)
    s = jnp.einsum("bcthd,bcshd->bctsh", qc, kc) * jnp.exp(d_log - m_t[:, :, :, None])
    inter_w = jnp.exp(inter_log - m_t)
    num = jnp.einsum("bctsh,bcshd->bcthd", s, vc) + inter_w[..., None] * jnp.einsum("bcthk,bchkv->bcthv", qc, C_in)
    den = jnp.sum(s, axis=3) + inter_w * jnp.einsum("bcthk,bchk->bcth", qc, n_in)
    h = num / jnp.maximum(jnp.abs(den), jnp.exp(-m_t))[..., None]
    return h.reshape(B, T, H, Dh)


def mlstm_branch(mq, mk, mv, mi, mf, mo, conv_w, b_i, b_f, norm_g):
    B, T, _ = mq.shape
    qk = jax.nn.silu(causal_depthwise_conv(jnp.concatenate([mq, mk], axis=-1), conv_w))
    q, k = jnp.split(qk, 2, axis=-1)
    shp = (B, T, MLSTM_HEADS, MLSTM_HEAD_DIM)
    h = mlstm_chunkwise(q.reshape(shp), k.reshape(shp), mv.reshape(shp),
                        (mi + b_i).astype(jnp.float32), (mf + b_f).astype(jnp.float32))
    mu = jnp.mean(h, axis=-1, keepdims=True)
    var = jnp.mean(jnp.square(h - mu), axis=-1, keepdims=True)
    h = ((h - mu) * lax.rsqrt(var + EPS)).reshape(B, T, MLSTM_W) * norm_g.astype(jnp.float32)
    return (jax.nn.sigmoid(mo.astype(jnp.float32)) * h).astype(mq.dtype)


def cross_attention(q, k, v):
    B, T, _ = q.shape
    M = k.shape[1]
    q = q.reshape(B, T, X_HEADS, X_HEAD_DIM)
    k = k.reshape(B, M, X_HEADS, X_HEAD_DIM)
    v = v.reshape(B, M, X_HEADS, X_HEAD_DIM)
    s = jnp.einsum("bthd,bmhd->bhtm", q, k).astype(jnp.float32) * X_HEAD_DIM ** -0.5
    p = jax.nn.softmax(s, axis=-1).astype(v.dtype)
    return jnp.einsum("bhtm,bmhd->bthd", p, v).reshape(B, T, X_W)


def setup_inputs(seed: int = 0) -> dict:
    key = jax.random.key(seed)
    ks = jax.random.split(key, 20)
    f32 = jnp.float32

    def nrm(k, shape, scale):
        return jax.random.normal(k, shape, f32) * scale

    def gain(k, shape):
        return 1.0 + 0.05 * jax.random.normal(k, shape, f32)

    f_bias = jnp.linspace(3.0, 6.0, MLSTM_HEADS, dtype=f32)[None] + 0.1 * jax.random.normal(ks[8], (DEPTH, MLSTM_HEADS), f32)
    return {
        "x": nrm(ks[0], (BATCH, SEQ, D_MODEL), 1.0),
        "mem": nrm(ks[1], (BATCH, MEM_LEN, D_MODEL), 1.0),
        "rel_bias": nrm(ks[2], (REL_BUCKETS, SWA_Q_HEADS), 0.5),
        "g_mix": gain(ks[3], (DEPTH, D_MODEL)),
        "w_in": nrm(ks[4], (DEPTH, D_MODEL, IN_COLS), D_MODEL ** -0.5),
        "conv_w": nrm(ks[5], (DEPTH, CONV_WIDTH, 2 * MLSTM_W), CONV_WIDTH ** -0.5),
        "b_i": nrm(ks[6], (DEPTH, MLSTM_HEADS), 0.1),
        "b_f": f_bias,
        "mlstm_norm_g": gain(ks[7], (DEPTH, MLSTM_W)),
        "sinks": nrm(ks[9], (DEPTH, SWA_Q_HEADS), 0.5),
        "g_mem": gain(ks[10], (DEPTH, D_MODEL)),
        "w_mem_kv": nrm(ks[11], (DEPTH, D_MODEL, 2 * X_W), D_MODEL ** -0.5),
        "w_br_swa": nrm(ks[12], (DEPTH, SWA_Q, D_MODEL), SWA_Q ** -0.5),
        "w_br_mlstm": nrm(ks[13], (DEPTH, MLSTM_W, D_MODEL), MLSTM_W ** -0.5),
        "w_br_x": nrm(ks[14], (DEPTH, X_W, D_MODEL), X_W ** -0.5),
        "w_out": nrm(ks[15], (DEPTH, D_MODEL, D_MODEL), D_MODEL ** -0.5),
        "g_ffn": gain(ks[16], (DEPTH, D_MODEL)),
        "w_ff1": nrm(ks[17], (DEPTH, D_MODEL, D_FF), D_MODEL ** -0.5),
        "w_ff2": nrm(ks[18], (DEPTH, D_FF, D_MODEL), D_FF ** -0.5),
        "g_final": gain(ks[19], (D_MODEL,)),
    }


def reference(x, mem, rel_bias, g_mix, w_in, conv_w, b_i, b_f, mlstm_norm_g, sinks, g_mem, w_mem_kv,
              w_br_swa, w_br_mlstm, w_br_x, w_out, g_ffn, w_ff1, w_ff2, g_final):
    B, T, _ = x.shape
    n_blocks = T // SWA_BLOCK
    band_bias, band_mask = swa_band_bias_and_mask(rel_bias, n_blocks)
    split_at = [int(c) for c in np.cumsum(IN_SIZES)[:-1]]
    for l in range(DEPTH):
        h = rms_norm(x, g_mix[l])
        proj = h @ w_in[l]
        sq, sk, sv, mq, mk, mv, mi, mf, mo, xq, gate_pre = jnp.split(proj, split_at, axis=-1)
        y_swa = swa_attention(sq, sk, sv, sinks[l], band_bias, band_mask)
        y_mlstm = mlstm_branch(mq, mk, mv, mi, mf, mo, conv_w[l], b_i[l], b_f[l], mlstm_norm_g[l])
        mem_kv = rms_norm(mem, g_mem[l]) @ w_mem_kv[l]
        mk_x, mv_x = jnp.split(mem_kv, 2, axis=-1)
        y_x = cross_attention(xq, mk_x, mv_x)
        gates = jax.nn.sigmoid(gate_pre.astype(jnp.float32)).astype(x.dtype).reshape(B, T, N_BRANCH, D_MODEL)
        merged = (gates[:, :, 0] * (y_swa @ w_br_swa[l])
                  + gates[:, :, 1] * (y_mlstm @ w_br_mlstm[l])
                  + gates[:, :, 2] * (y_x @ w_br_x[l]))
        x = x + merged @ w_out[l]
        h = rms_norm(x, g_ffn[l])
        x = x + jnp.square(jax.nn.relu(h @ w_ff1[l])) @ w_ff2[l]
    return rms_norm(x, g_final)
```

```python
import math
import os
import types
import numpy as np
import concourse.bass as bass
import concourse.mybir as mybir
from concourse.bass_utils import run_bass_kernel_spmd

F32 = mybir.dt.float32
BF16 = mybir.dt.bfloat16
AF = mybir.ActivationFunctionType
ALU = mybir.AluOpType
AX = mybir.AxisListType

PE, ACT, DVE, POOL, SP = "pe", "act", "dve", "pool", "sp"
COMPUTE = (PE, ACT, DVE, POOL)

D = 1024
T = 2048
DEPTH = 2
MEM = 256
NCOL = 6416
TG = 512
NTG = T // TG
EPS = 1e-6
NEG = -30000.0
C_SQ, C_SK, C_SV, C_MQ, C_MK, C_MV, C_MI, C_MF, C_MO, C_XQ, C_G = 0, 512, 640, 768, 1280, 1792, 2304, 2312, 2320, 2832, 3344


def _freeze(fn):
    if fn.__closure__ is None:
        return fn
    cells = []
    for c in fn.__closure__:
        try:
            cells.append(types.CellType(c.cell_contents))
        except ValueError:
            cells.append(c)
    return types.FunctionType(fn.__code__, fn.__globals__, fn.__name__, fn.__defaults__, tuple(cells))


class Prog:
    def __init__(self, nc):
        self.nc = nc
        self.q = {e: [] for e in (PE, ACT, DVE, POOL, SP)}
        self.cnt = {e: 0 for e in COMPUTE}
        self.sems = {e: nc.alloc_semaphore("s_" + e) for e in COMPUTE}
        self.dsem = {}
        self.waited = {e: {} for e in self.q}
        self.res = {}
        self.nwaits = 0

    def _need(self, e, tok, waits, raw):
        if tok is None:
            return
        src, sem, val = tok
        if src == e and src != "dma":
            if e == PE:
                return
        key = sem.num
        if self.waited[e].get(key, 0) >= val:
            return
        cur = waits.get(key)
        if cur is None or cur[1] < val:
            waits[key] = (sem, val)

    def op(self, e, fn, reads=(), writes=(), mark=True, dma_res=None, ndma=1):
        waits = {}
        for r in reads:
            st = self.res.get(r)
            if st is not None:
                self._need(e, st[0], waits, True)
        for w in writes:
            st = self.res.get(w)
            if st is not None:
                self._need(e, st[0], waits, False)
                for t in st[1]:
                    self._need(e, t, waits, False)
        wl = list(waits.values())
        for sem, val in wl:
            self.waited[e][sem.num] = val
        self.nwaits += len(wl)
        if dma_res is not None:
            sem, c = self.dsem.get(dma_res, (None, 0))
            if sem is None:
                sem = self.nc.alloc_semaphore("d%d" % len(self.dsem))
            c += 16 * ndma
            self.dsem[dma_res] = (sem, c)
            tok = ("dma", sem, c)
            inc = (sem, 16)
        elif mark:
            self.cnt[e] += 1
            tok = (e, self.sems[e], self.cnt[e])
            inc = (self.sems[e], 1)
        else:
            tok = (e, self.sems[e], self.cnt[e] + 1)
            inc = None
        for r in reads:
            st = self.res.setdefault(r, [None, []])
            st[1].append(tok)
            if len(st[1]) > 24:
                st[1] = self._compact(st[1])
        for w in writes:
            self.res[w] = [tok, []]
        self.q[e].append((wl, _freeze(fn), inc))
        return tok

    @staticmethod
    def _compact(toks):
        best = {}
        for t in toks:
            k = (t[0], t[1].num)
            if k not in best or best[k][2] < t[2]:
                best[k] = t
        return list(best.values())

    def barrier(self):
        toks = [(e, self.sems[e], self.cnt[e]) for e in COMPUTE if self.cnt[e] > 0]
        toks += [("dma", sem, c) for (sem, c) in self.dsem.values()]
        for e in self.q:
            waits = {}
            for t in toks:
                if t[0] == e:
                    continue
                self._need(e, t, waits, True)
            wl = list(waits.values())
            for sem, val in wl:
                self.waited[e][sem.num] = val
            self.q[e].append((wl, None, None))

    def final_wait(self, e, toks):
        waits = {}
        for t in toks:
            self._need(e, t, waits, True)
        self.q[e].append((list(waits.values()), None, None))

    def emit(self):
        nc = self.nc
        with nc.Block() as block:
            def run(e):
                def body(engine):
                    for wl, fn, inc in self.q[e]:
                        for sem, val in wl:
                            engine.wait_ge(sem, val)
                        if fn is None:
                            continue
                        ins = fn(engine)
                        if inc is not None:
                            if isinstance(ins, (list, tuple)):
                                for i_ in ins:
                                    i_.then_inc(inc[0], inc[1])
                            else:
                                ins.then_inc(inc[0], inc[1])
                return body
            block.tensor(run(PE))
            block.scalar(run(ACT))
            block.vector(run(DVE))
            block.gpsimd(run(POOL))
            block.sync(run(SP))


def t5_bucket(d):
    if d < 16:
        return d
    v = 16 + int(np.float32(np.log(np.float32(d) / np.float32(16)) * np.float32(16 / math.log(128 / 16))))
    return min(v, 31)


def host_consts():
    c = {}
    c["ident"] = np.eye(128, dtype=np.float32)
    c["jrev"] = np.eye(128, dtype=np.float32)[::-1].copy()
    oh = np.zeros((33, 384), np.float32)
    for j in range(384):
        d = j - 128
        if 0 <= d < 128:
            oh[t5_bucket(d), j] = 1.0
        else:
            oh[32, j] = NEG
    c["ohm"] = oh
    s = np.arange(128)[:, None]
    t = np.arange(128)[None, :]
    c["mask01"] = (s <= t).astype(np.float32)
    sel = np.zeros((8, 128), np.float32)
    for h in range(8):
        sel[h, (h % 2) * 64:(h % 2) * 64 + 64] = 1.0
    c["sel"] = sel
    pm = np.zeros((8, 4), np.float32)
    for h in range(8):
        pm[h, h // 2] = 1.0
    c["pm"] = pm
    return c


class _Stop(Exception):
    pass


def build(debug=(), stop=None):
    def chk(name):
        if stop == name:
            raise _Stop()
    nc = bass.Bass("TRN2", target_bir_lowering=False)
    P = Prog(nc)

    def din(name, shape):
        return nc.dram_tensor(name, list(shape), F32, kind="ExternalInput")

    x_d = din("x", [T, D]); mem_d = din("mem", [MEM, D]); relb_d = din("rel_bias", [32, 8])
    gmix_d = din("g_mix", [DEPTH, D]); win_d = din("w_in", [DEPTH, D, NCOL]); convw_d = din("conv_w", [DEPTH, 4, 1024])
    bi_d = din("b_i", [DEPTH, 8]); bf_d = din("b_f", [DEPTH, 8]); mng_d = din("mlstm_norm_g", [DEPTH, 512])
    sinks_d = din("sinks", [DEPTH, 8]); gmem_d = din("g_mem", [DEPTH, D]); wmkv_d = din("w_mem_kv", [DEPTH, D, 1024])
    wbs_d = din("w_br_swa", [DEPTH, 512, D]); wbm_d = din("w_br_mlstm", [DEPTH, 512, D]); wbx_d = din("w_br_x", [DEPTH, 512, D])
    wout_d = din("w_out", [DEPTH, D, D]); gffn_d = din("g_ffn", [DEPTH, D]); wff1_d = din("w_ff1", [DEPTH, D, 4096])
    wff2_d = din("w_ff2", [DEPTH, 4096, D]); gfin_d = din("g_final", [D])
    ident_d = din("ident", [128, 128]); jrev_d = din("jrev", [128, 128]); ohm_d = din("ohm", [33, 384])
    mask_d = din("mask01", [128, 128]); sel_d = din("sel", [8, 128]); pm_d = din("pm", [8, 4])
    out_d = nc.dram_tensor("out", [T, D], F32, kind="ExternalOutput")
    ext_d = nc.dram_tensor("ext_scratch", [8, 384], F32)
    dbg_out = {}

    alloc = {"off": int(nc.sbuf_base), "top": int(nc.sbuf_top)}

    def sb(name, shape, dt=F32):
        n = 1
        for s_ in shape[1:]:
            n *= s_
        nb = (n * mybir.dt.size(dt) + 31) // 32 * 32
        off = alloc["off"]
        assert off + nb <= alloc["top"], ("SBUF overflow", name, off, nb, alloc["top"])
        t = nc.alloc_sbuf_tensor_at(name, list(shape), dt, offset=off)
        alloc["off"] = off + nb
        return t

    def AP(t, off, dims, p0=0, npart=128):
        fs = 1
        for s_ in t.shape[1:]:
            fs *= s_
        return bass.AP(t, p0 * fs + off, [[fs, npart]] + [list(d_) for d_ in dims])

    alloc["off"] = (alloc["off"] + 31) // 32 * 32
    xT = sb("xT", [128, 8, T])
    wbuf = [sb("wbuf%d" % i, [128, 4096], BF16) for i in range(3)]
    biasT = sb("biasT", [128, 8, 2, 128])
    sc_sb = [sb("sc_sb%d" % i, [128, 512]) for i in range(2)]
    sqb = sc_sb
    rstd = sb("rstd", [128, 512])
    ident = sb("ident_sb", [128, 128]); ident_bf = sb("ident_bf", [128, 128], BF16)
    mask01 = sb("mask01_sb", [128, 128])
    ones_f = sb("ones_f", [128, 128]); ones_bf = sb("ones_bf", [128, 128], BF16)
    sel_sb = sb("sel_sb", [8, 128]); pm_sb = sb("pm_sb", [8, 4])
    epscol = sb("epscol", [128, 1])
    normg_bc = sb("normg_bc", [128, 512])
    gcols = sb("gcols", [128, 7, 8])
    cw = sb("cw", [128, DEPTH, 8, 4])
    bif = sb("bif", [8, 8])
    sinkexp = sb("sinkexp", [128, DEPTH, 8])
    kxT = sb("kxT", [128, DEPTH, 4, MEM], BF16); vx = sb("vx", [128, DEPTH, 2, 512], BF16)
    scope_mark = alloc["off"]
    jrev = sb("jrev_sb", [128, 128])
    lhs33 = sb("lhs33", [33, 8]); ohm_sb = sb("ohm_sb", [33, 384]); ext_sb = sb("ext_sb", [8, 384])
    hks = [sb("hk%d" % i, [128, 512]) for i in range(4)]
    xsb = [sb("xs%d" % i, [128, 1024]) for i in range(3)]
    memT = sb("memT", [128, 8, MEM]); memnT = sb("memnT", [128, 8, MEM], BF16)
    alloc["off"] = scope_mark
    xs_f = [sb("xs_f%d" % i, [128, 1024]) for i in range(2)]
    gfin_bc = sb("gfin_bc", [128, 1024])
    fsq = sb("fsq", [128, 1024])
    fss = sb("fss", [128, 16])
    alloc["off"] = scope_mark
    hT = sb("hT", [128, 8, TG], BF16)
    hTb = sb("hTb", [128, 8, TG], BF16)
    arena = sb("arena", [128, 32 * 512], BF16)
    kz = sb("kz", [128, 2, 2, 640], BF16)
    vswa = sb("vswa", [128, 5, 2, 65], BF16)
    ysw_tok = sb("ysw_tok", [128, 512], BF16)
    swt = sb("swt", [128, 8])
    yx_tok = sb("yx_tok", [128, 4, 128], BF16)
    stage = [sb("stage%d" % i, [128, 544]) for i in range(2)]
    carry = sb("carry", [128, 8, 4])
    caccs = [sb("cacc%d" % i, [128, 512]) for i in range(2)]
    ktok = sb("ktok", [128, 512], BF16)
    vtok = sb("vtok", [128, 4, 8, 65], BF16)
    og = sb("og", [128, 4, 512], BF16)
    sqh_ap = AP(sc_sb[0], 0, [[64, 8], [1, 64]])
    STm = sb("STm", [128, 8, 128], BF16)
    vu = sb("vu", [128, 8, 65], BF16)
    hh = sb("hh", [128, 8, 64])
    ytok = sb("ytok", [128, 512], BF16)
    lnt = sb("lnt", [128, 8, 8])
    Cst = sb("Cst", [128, 4, 65])
    Cd32 = sb("Cd32", [128, 4, 65])
    Cd16z = sb("Cd16z", [128, 8, 65], BF16)
    Rrows = sb("Rrows", [128, 512])
    r_small = sb("r_small", [8, 64])
    utok = sb("utok", [128, 4, 8]); ftok = sb("ftok", [128, 4, 8])
    decay_bc = sb("decay_bc", [128, 4, 4])
    sg = sb("sg", [128, 3, 512], BF16)
    mtmp = [sb("mtmp%d" % i, [128, 512]) for i in range(2)]
    r32 = mtmp
    r_ia = mtmp[0][0:8, :]; r_sp = mtmp[1][0:8, :]; r_F = sc_sb[0][0:8, :]; r_M = sc_sb[1][0:8, :]
    print("SBUF main scope end", alloc["off"], "top", alloc["top"])
    ps = [nc.alloc_psum_tensor("ps%d" % i, [128, 512], F32) for i in range(8)]

    out_toks = []
    stopped = False
    try:
        def ar(chunk, n=1, p0=0, npart=128, sub=None):
            return AP(arena, chunk * 512, [[512, n], [1, 512]], p0, npart)
        A_QK, A_Q, A_XQ, A_YS, A_MG, A_YM, A_YX = 0, 8, 12, 16, 0, 24, 28

        def arn(c):
            return "ar%d" % c

        def dma(e, out, in_, writes, reads=(), res=None, **kw):
            return P.op(e, lambda eng: eng.dma_start(out=out, in_=in_, **kw), reads=reads, writes=writes,
                        dma_res=res or writes[0])

        rr = {"ev": 0}

        def evac_eng():
            rr["ev"] ^= 1
            return ACT if rr["ev"] else DVE

        def copy(e, out, in_, reads, writes):
            if e == ACT:
                return P.op(ACT, lambda eng: eng.activation(out=out, in_=in_, func=AF.Copy), reads=reads, writes=writes)
            return P.op(e, lambda eng: eng.tensor_copy(out=out, in_=in_), reads=reads, writes=writes)

        def mm(out, lhsT, rhs, start, stop, reads, writes, mark=None):
            return P.op(PE, lambda eng: eng.matmul(out, lhsT, rhs, start=start, stop=stop), reads=reads, writes=writes,
                        mark=stop if mark is None else mark)

        wq = []
        wstate = {"issued": 0, "used": 0}

        def wissue():
            i = wstate["issued"]
            if i >= len(wq):
                return
            name, fn = wq[i]
            b = i % 3
            pairs = fn(wbuf[b])
            P.op(POOL, lambda eng: [eng.dma_start(out=o, in_=s_) for o, s_ in pairs], writes=["wbuf%d" % b],
                 dma_res="wbuf%d" % b, ndma=len(pairs))
            wstate["issued"] += 1

        def wnext(name, prefetch=True):
            i = wstate["used"]
            assert wq[i][0] == name, (wq[i][0], name)
            assert wstate["issued"] > i or prefetch
            while prefetch and wstate["issued"] < min(i + 3, len(wq)):
                wissue()
            wstate["used"] += 1
            return wbuf[i % 3], "wbuf%d" % (i % 3)

        def wAP(t, off, dims):
            return bass.AP(t, off, [list(d_) for d_ in dims])

        def blk_k1024(wd, base, rowlen, col0, ncols):
            def fn(buf):
                return [(AP(buf, 0, [[ncols, 8], [1, ncols]]),
                         wAP(wd, base + col0, [[rowlen, 128], [128 * rowlen, 8], [1, ncols]]))]
            return fn

        def blk_b1(l):
            base = l * D * NCOL

            def fn(buf):
                prs = []
                for kv in range(2):
                    for dup in range(2):
                        prs.append((AP(buf, kv * 128 + dup * 64, [[400, 8], [1, 64]]),
                                    wAP(win_d, base + C_SK + kv * 64, [[NCOL, 128], [128 * NCOL, 8], [1, 64]])))
                prs.append((AP(buf, 256, [[400, 8], [1, 128]]),
                            wAP(win_d, base + C_SV, [[NCOL, 128], [128 * NCOL, 8], [1, 128]])))
                prs.append((AP(buf, 384, [[400, 8], [1, 16]]),
                            wAP(win_d, base + C_MI, [[NCOL, 128], [128 * NCOL, 8], [1, 16]])))
                return prs
            return fn

        def blk_gate(l, j):
            base = l * D * NCOL

            def fn(buf):
                return [(AP(buf, b * 128, [[384, 8], [1, 128]]),
                         wAP(win_d, base + C_G + b * 1024 + j * 128, [[NCOL, 128], [128 * NCOL, 8], [1, 128]]))
                        for b in range(3)]
            return fn

        def blk_br(l, j):
            def fn(buf):
                prs = [(AP(buf, 0, [[128, 4], [1, 128]]),
                        wAP(wbs_d, l * 512 * D + j * 128, [[D, 128], [128 * D, 4], [1, 128]]))]
                prs.append((AP(buf, 512, [[128, 4], [1, 128]]),
                            wAP(wbm_d, l * 512 * D + j * 128, [[D, 128], [128 * D, 4], [1, 128]])))
                prs.append((AP(buf, 1024, [[128, 4], [1, 128]]),
                            wAP(wbx_d, l * 512 * D + j * 128, [[D, 128], [128 * D, 4], [1, 128]])))
                return prs
            return fn

        def blk_ff2(l, dc):
            def fn(buf):
                return [(AP(buf, 0, [[128, 32], [1, 128]]),
                         wAP(wff2_d, l * 4096 * D + dc * 128, [[D, 128], [128 * D, 32], [1, 128]]))]
            return fn

        for l in range(DEPTH):
            wq.append(("memk%d" % l, blk_k1024(wmkv_d, l * D * 1024, 1024, 0, 512)))
            wq.append(("memv%d" % l, blk_k1024(wmkv_d, l * D * 1024, 1024, 512, 512)))
        for l in range(DEPTH):
            for g in range(NTG):
                tag = "%d_%d" % (l, g)
                wq.append(("b1" + tag, blk_b1(l)))
                wq.append(("sq" + tag, blk_k1024(win_d, l * D * NCOL, NCOL, C_SQ, 512)))
                wq.append(("mq" + tag, blk_k1024(win_d, l * D * NCOL, NCOL, C_MQ, 512)))
                wq.append(("mv" + tag, blk_k1024(win_d, l * D * NCOL, NCOL, C_MV, 512)))
                wq.append(("mk" + tag, blk_k1024(win_d, l * D * NCOL, NCOL, C_MK, 512)))
                wq.append(("mo" + tag, blk_k1024(win_d, l * D * NCOL, NCOL, C_MO, 512)))
                wq.append(("xq" + tag, blk_k1024(win_d, l * D * NCOL, NCOL, C_XQ, 512)))
                for j in range(8):
                    wq.append(("g%d_" % j + tag, blk_gate(l, j)))
                    wq.append(("br%d_" % j + tag, blk_br(l, j)))
                for i in range(2):
                    wq.append(("wo%d_" % i + tag, blk_k1024(wout_d, l * D * D, D, i * 512, 512)))
                for i in range(8):
                    wq.append(("f1%d_" % i + tag, blk_k1024(wff1_d, l * D * 4096, 4096, i * 512, 512)))
                for i in range(8):
                    wq.append(("f2%d_" % i + tag, blk_ff2(l, i)))

        dma(SP, ident[:], ident_d.ap(), ["ident"])
        dma(POOL, ohm_sb[:], ohm_d.ap(), ["ohm"])
        P.op(DVE, lambda e: e.memset(lhs33[:], 1.0), writes=["lhs33"])
        dma(POOL, lhs33[0:32, :], relb_d.ap(), ["lhs33"], res="relb")
        dma(POOL, jrev[:], jrev_d.ap(), ["jrev"])
        dma(POOL, mask01[:], mask_d.ap(), ["mask01"])
        dma(POOL, sel_sb[:], sel_d.ap(), ["sel"])
        dma(POOL, pm_sb[:], pm_d.ap(), ["pm"])
        copy(DVE, ident_bf[:], ident[:], ["ident"], ["ident_bf"])
        P.op(DVE, lambda e: e.memset(ones_f[:], 1.0), writes=["ones_f"])
        P.op(DVE, lambda e: e.memset(epscol[:], EPS), writes=["epscol"])
        P.op(DVE, lambda e: e.memset(ones_bf[:], 1.0), writes=["ones_bf"])
        gsrcs = [(gmix_d, 0), (gmix_d, D), (gffn_d, 0), (gffn_d, D), (gmem_d, 0), (gmem_d, D), (gfin_d, 0)]
        for i, (gd, off) in enumerate(gsrcs):
            dma(ACT, gcols[:, i, :], wAP(gd, off, [[1, 128], [128, 8]]), ["gcols"], res="gcols%d" % i,
                allow_slow_non_contiguous=True)
        for l in range(DEPTH):
            dma(ACT, sinkexp[:, l, :], wAP(sinks_d, l * 8, [[0, 128], [1, 8]]), ["sinkraw"], res="sink%d" % l)
            dma(ACT, bif[:, l:l + 1], wAP(bi_d, l * 8, [[1, 8], [1, 1]]), ["bif"], res="bi%d" % l)
            dma(ACT, bif[:, 2 + l:3 + l], wAP(bf_d, l * 8, [[1, 8], [1, 1]]), ["bif"], res="bf%d" % l)
        P.op(ACT, lambda e: e.activation(out=sinkexp[:], in_=sinkexp[:], func=AF.Exp), reads=["sinkraw"], writes=["sinkexp"])
        P.op(DVE, lambda e: e.tensor_scalar(out=bif[:, 4:6], in0=bif[:, 2:4], scalar1=-1.0, scalar2=None, op0=ALU.mult),
             reads=["bif"], writes=["nbf"])
        mm(ps[0][0:8, 0:384], lhs33[:], ohm_sb[:], True, True, ["lhs33", "ohm"], ["ps0"])
        copy(DVE, ext_sb[:], ps[0][0:8, 0:384], ["ps0"], ["ext_sb"])
        dma(POOL, ext_d.ap(), ext_sb[:], ["ext_d"], reads=["ext_sb"])
        for half in range(2):
            for h4 in range(2):
                off = 129 if half == 0 else 1
                bi_ = half * 2 + h4
                for hl in range(4):
                    h = h4 * 4 + hl
                    dma(POOL, hks[bi_][:, hl * 128:(hl + 1) * 128], wAP(ext_d, h * 384 + off, [[1, 128], [1, 128]]), ["hk%d_%d" % (bi_, hl)],
                        reads=["ext_d"])

        chk('consts')
        for i in range(T // 128):
            xs = xsb[i % 3]
            xsn = "xs%d" % (i % 3)
            dma(SP, xs[:], x_d.ap()[i * 128:(i + 1) * 128, :], [xsn])
            for half in range(2):
                pb = ps[2 + (2 * i + half) % 4]
                pn = "ps%d" % (2 + (2 * i + half) % 4)
                for j in range(4):
                    c = half * 4 + j
                    P.op(PE, lambda e, pb=pb, j=j, c=c: e.transpose(pb[:, j * 128:(j + 1) * 128], xs[:, c * 128:(c + 1) * 128], ident[:]),
                         reads=[xsn, "ident"], writes=[pn], mark=(j == 3))
                copy(DVE, AP(xT, half * 4 * T + i * 128, [[T, 4], [1, 128]]), AP(pb, 0, [[128, 4], [1, 128]]),
                     [pn], ["xT%d_%d" % (c_, i // 4) for c_ in range(half * 4, half * 4 + 4)])
        for i in range(MEM // 128):
            xs = xsb[(i + 1) % 3]
            xsn = "xs%d" % ((i + 1) % 3)
            dma(SP, xs[:], mem_d.ap()[i * 128:(i + 1) * 128, :], [xsn])
            for half in range(2):
                pb = ps[2 + (2 * i + half) % 4]
                pn = "ps%d" % (2 + (2 * i + half) % 4)
                for j in range(4):
                    c = half * 4 + j
                    P.op(PE, lambda e, pb=pb, j=j, c=c: e.transpose(pb[:, j * 128:(j + 1) * 128], xs[:, c * 128:(c + 1) * 128], ident[:]),
                         reads=[xsn, "ident"], writes=[pn], mark=(j == 3))
                copy(DVE, AP(memT, half * 4 * MEM + i * 128, [[MEM, 4], [1, 128]]), AP(pb, 0, [[128, 4], [1, 128]]),
                     [pn], ["memT"])

        for l in range(DEPTH):
            for c in range(8):
                dma(SP, cw[:, l, c, :], wAP(convw_d, l * 4096 + c * 128, [[1, 128], [1024, 4]]), ["cw"],
                    res="cw%d_%d" % (l, c), allow_slow_non_contiguous=True)
        for half in range(2):
            for h4 in range(2):
                bi_ = half * 2 + h4
                mm(ps[1][:], jrev[:], hks[bi_][:], True, True, ["jrev"] + ["hk%d_%d" % (bi_, hl) for hl in range(4)], ["ps1"])
                copy(DVE, AP(biasT, h4 * 4 * 256 + half * 128, [[256, 4], [1, 128]]),
                     AP(ps[1], 0, [[128, 4], [1, 128]]), ["ps1"], ["biasT"])
        chk('load')
        def rmsnorm(src_fn, src_res_fn, gi, dst_fn, dst_res_fn, ntok, psb):
            pn = "ps%d" % psb
            srcs = [src_fn(c) for c in range(8)]
            dsts = [dst_fn(c) for c in range(8)]
            for c in range(8):
                sq = sqb[c % 2]
                P.op(ACT, lambda e, c=c, sq=sq: e.activation(out=sq[:, 0:ntok], in_=srcs[c], func=AF.Square),
                     reads=[src_res_fn(c)], writes=["sc_sb%d" % (c % 2)])
                mm(ps[psb][:, 0:ntok], ones_f[:], sq[:, 0:ntok], c == 0, c == 7, ["sc_sb%d" % (c % 2), "ones_f"], [pn], mark=True)
            P.op(ACT, lambda e: e.activation(out=rstd[:, 0:ntok], in_=ps[psb][:, 0:ntok], func=AF.Ln, scale=1.0 / D, bias=epscol[:, 0:1]),
                 reads=[pn, "epscol"], writes=["rstd"])
            P.op(ACT, lambda e: e.activation(out=rstd[:, 0:ntok], in_=rstd[:, 0:ntok], func=AF.Exp, scale=-0.5), reads=["rstd"], writes=["rstd"])
            for c in range(8):
                eng = DVE
                P.op(eng, lambda e, c=c: e.scalar_tensor_tensor(out=dsts[c], in0=srcs[c], scalar=gcols[:, gi, c:c + 1],
                                                                in1=rstd[:, 0:ntok], op0=ALU.mult, op1=ALU.mult),
                     reads=[src_res_fn(c), "rstd", "gcols"], writes=[dst_res_fn(c)])

        def norm_sq(c, src_ap, src_res, psb, ntok):
            sq = sqb[c % 2]
            P.op(ACT, lambda e: e.activation(out=sq[:, 0:ntok], in_=src_ap, func=AF.Square), reads=[src_res], writes=["sc_sb%d" % (c % 2)])
            mm(ps[psb][:, 0:ntok], ones_f[:], sq[:, 0:ntok], c == 0, c == 7, ["sc_sb%d" % (c % 2), "ones_f"], ["ps%d" % psb], mark=True)

        def norm_fin(srcs, src_res, gi, dsts, dst_res, ntok, psb):
            pn = "ps%d" % psb
            P.op(ACT, lambda e: e.activation(out=rstd[:, 0:ntok], in_=ps[psb][:, 0:ntok], func=AF.Ln, scale=1.0 / D, bias=epscol[:, 0:1]),
                 reads=[pn, "epscol"], writes=["rstd"])
            P.op(ACT, lambda e: e.activation(out=rstd[:, 0:ntok], in_=rstd[:, 0:ntok], func=AF.Exp, scale=-0.5), reads=["rstd"], writes=["rstd"])
            for c in range(8):
                P.op(DVE, lambda e, c=c: e.scalar_tensor_tensor(out=dsts[c], in0=srcs[c], scalar=gcols[:, gi, c:c + 1],
                                                                in1=rstd[:, 0:ntok], op0=ALU.mult, op1=ALU.mult),
                     reads=[src_res[c], "rstd", "gcols"], writes=[dst_res[c]])

        def mixer_norm(l_, g_, psb):
            t_ = g_ * TG
            rmsnorm(lambda c: xT[:, c, t_:t_ + TG], lambda c: xres(c, g_), l_, lambda c: hT[:, c, :], lambda c: "hT%d" % c, TG, psb)

        def xres(c, g):
            return "xT%d_%d" % (c, g)

        def dbg(name, ap, shape, reads):
            if name in debug:
                d = nc.dram_tensor("dbg_" + name, list(shape), ap.dtype, kind="ExternalOutput")
                dbg_out[name] = d
                dma(SP, d.ap(), ap, ["dbg_" + name], reads=reads)

        for l in range(DEPTH):
            rmsnorm(lambda c: memT[:, c, :], lambda c: "memT", 4 + l, lambda c: memnT[:, c, :], lambda c: "memnT", MEM, 0)
            wb, wn = wnext("memk%d" % l)
            for hx in range(4):
                pb = 1 + hx % 2
                for kc in range(8):
                    mm(ps[pb][:, 0:MEM], AP(wb, kc * 512 + hx * 128, [[1, 128]]), memnT[:, kc, :], kc == 0, kc == 7,
                       [wn, "memnT"], ["ps%d" % pb])
                copy(evac_eng(), kxT[:, l, hx, :], ps[pb][:, 0:MEM], ["ps%d" % pb], ["kxT"])
            wb, wn = wnext("memv%d" % l)
            for mt in range(2):
                pb = 3 + mt
                for kc in range(8):
                    mm(ps[pb][:], memnT[:, kc, mt * 128:(mt + 1) * 128], AP(wb, kc * 512, [[1, 512]]), kc == 0, kc == 7,
                       [wn, "memnT"], ["ps%d" % pb])
                copy(evac_eng(), vx[:, l, mt, :], ps[pb][:], ["ps%d" % pb], ["vx"])
        chk('memkv')
        P.barrier()
        P.op(DVE, lambda e: e.memset(Rrows[:], 0.0), writes=["Rrows"])
        P.op(POOL, lambda e: e.memset(vtok[:], 1.0), writes=["vtok0", "vtok1", "vtok2", "vtok3"])
        for l in range(DEPTH):
            dma(SP, normg_bc[:], wAP(mng_d, l * 512, [[0, 128], [1, 512]]), ["normg"])
            P.op(POOL, lambda e: e.memset(carry[:], 0.0), writes=["carry%d" % c for c in range(8)])
            P.op(POOL, lambda e: e.memset(kz[:], 0.0), writes=["kdup"])
            P.op(POOL, lambda e: e.memset(Cd16z[:], 0.0), writes=["Cd16"])
            P.op(POOL, lambda e: e.memset(vswa[:], 1.0), writes=["vswa"])
            P.op(POOL, lambda e: e.memset(Cst[:], 0.0), writes=["Cst"])
            P.op(POOL, lambda e: e.memset(r_small[:, 24:26], 0.0), writes=["FMc"])

            for g in range(NTG):
                tag = "%d_%d" % (l, g)
                t0 = g * TG
                if l == 0 and g == 0:
                    mixer_norm(0, 0, 0)
                hres = ["hT%d" % c for c in range(8)]
                if l == 0 and g == 0:
                    dbg("hT", hT[:], [128, 8, TG], hres)

                def projB(wb, wn, coloff, stride, ncols, pb, p_cols=TG):
                    for kc in range(8):
                        mm(ps[pb][0:ncols, 0:TG], AP(wb, kc * stride + coloff, [[1, ncols]]), hT[:, kc, :], kc == 0, kc == 7,
                           [wn] + hres, ["ps%d" % pb])

                def projA(wb, wn, coloff, stride, ncols, tt, pb):
                    for kc in range(8):
                        mm(ps[pb][:, 0:ncols], hT[:, kc, tt * 128:(tt + 1) * 128], AP(wb, kc * stride + coloff, [[1, ncols]]),
                           kc == 0, kc == 7, [wn] + hres, ["ps%d" % pb])

                if l == 0 and g == 0:
                    dbg('biasT', biasT[:], [128, 8, 2, 128], ['biasT'])
                chk('norm')
                wb, wn = wnext("b1" + tag)
                for kv in range(2):
                    pb = 4 + kv
                    projB(wb, wn, kv * 128, 400, 128, pb)
                    copy(ACT, kz[0:64, kv, 0, 128:640], ps[pb][0:64, :], ["ps%d" % pb], ["kdup"])
                    copy(DVE, kz[64:128, kv, 1, 128:640], ps[pb][64:128, :], ["ps%d" % pb], ["kdup"])
                for tt in range(4):
                    pb = 6 + tt % 2
                    projA(wb, wn, 256, 400, 128, tt, pb)
                    copy(evac_eng(), AP(vswa, (1 + tt) * 130, [[65, 2], [1, 64]]), AP(ps[pb], 0, [[64, 2], [1, 64]]), ["ps%d" % pb], ["vswa"])
                projB(wb, wn, 384, 400, 8, 0)
                projB(wb, wn, 392, 400, 8, 1)
                P.op(ACT, lambda e: e.activation(out=r_ia, in_=ps[0][0:8, :], func=AF.Identity, bias=bif[:, l:l + 1]),
                     reads=["ps0", "bif"], writes=["mtmp0"])
                P.op(ACT, lambda e: e.activation(out=r_sp, in_=ps[1][0:8, :], func=AF.Exp, bias=bif[:, 4 + l:5 + l], scale=-1.0),
                     reads=["ps1", "nbf"], writes=["mtmp1"])
                P.op(ACT, lambda e: e.activation(out=r_sp, in_=r_sp, func=AF.Ln, bias=ones_f[0:8, 0:1]), reads=["mtmp1", "ones_f"],
                     writes=["mtmp1"])
                P.op(DVE, lambda e: e.tensor_tensor_scan(out=r_F, data0=AP(ones_f, 0, [[0, 512]], 0, 8), data1=r_sp, initial=r_small[:, 24:25],
                                                         op0=ALU.mult, op1=ALU.add), reads=["mtmp1", "ones_f", "FMc"], writes=["sc_sb0"])
                P.op(DVE, lambda e: e.tensor_tensor(out=r_ia, in0=r_ia, in1=r_F, op=ALU.add),
                     reads=["mtmp0", "sc_sb0"], writes=["mtmp0"])
                P.op(DVE, lambda e: e.tensor_tensor_scan(out=r_M, data0=AP(ones_f, 0, [[0, 512]], 0, 8), data1=r_ia, initial=r_small[:, 25:26],
                                                         op0=ALU.mult, op1=ALU.max), reads=["mtmp0", "ones_f", "FMc"], writes=["sc_sb1"])
                for c in range(4):
                    P.op(DVE, lambda e, c=c: e.tensor_scalar(out=r_small[:, c:c + 1], in0=r_M[:, 128 * c + 127:128 * c + 128],
                                                             scalar1=-1.0, scalar2=None, op0=ALU.mult), reads=["sc_sb1"], writes=["nme"])
                for c in range(4):
                    cs = slice(c * 128, (c + 1) * 128)
                    P.op(ACT, lambda e, c=c, cs=cs: e.activation(out=Rrows[0:8, cs], in_=r_ia[:, cs], func=AF.Exp,
                                                                 bias=r_small[:, c:c + 1]), reads=["mtmp0", "nme"], writes=["Rrows"])
                    P.op(ACT, lambda e, c=c, cs=cs: e.activation(out=Rrows[32:40, cs], in_=r_F[:, c * 128:128 + c * 128], func=AF.Exp,
                                                                 bias=r_small[:, c:c + 1]), reads=["sc_sb0", "nme"], writes=["Rrows"])
                    min_ap = r_M[:, 128 * c - 1:128 * c] if c > 0 else r_small[:, 25:26]
                    P.op(ACT, lambda e, c=c, min_ap=min_ap: e.activation(out=r_small[:, 4 + c:5 + c], in_=min_ap, func=AF.Exp,
                                                                         bias=r_small[:, c:c + 1]), reads=["sc_sb1", "FMc", "nme"], writes=["decay"])
                    P.op(DVE, lambda e, c=c: e.tensor_scalar(out=r_small[:, 8 + 4 * c:12 + 4 * c], in0=pm_sb[:], scalar1=r_small[:, 4 + c:5 + c],
                                                             scalar2=None, op0=ALU.mult), reads=["decay", "pm"], writes=["Dg"])
                wb, wn = wnext("sq" + tag)
                for c in range(4):
                    pb = c % 4
                    projB(wb, wn, c * 128, 512, 128, pb)
                    copy(evac_eng(), ar(A_Q + c)[:, 0, :], ps[pb][:], ["ps%d" % pb], [arn(A_Q + c)])
                mm(ps[2][:, 0:16], sel_sb[:], r_small[:, 8:24], True, True, ["sel", "Dg"], ["ps2"])
                copy(DVE, decay_bc[:], AP(ps[2], 0, [[4, 4], [1, 4]]), ["ps2"], ["decay_bc"])
                for tt in range(4):
                    P.op(PE, lambda e, tt=tt: e.transpose(ps[3][:, 0:64], Rrows[0:64, tt * 128:(tt + 1) * 128], ident[0:64, 0:64]),
                         reads=["Rrows", "ident"], writes=["ps3"])
                    copy(DVE, utok[:, tt, :], ps[3][:, 0:8], ["ps3"], ["utok"])
                    copy(ACT, ftok[:, tt, :], ps[3][:, 32:40], ["ps3"], ["ftok"])
                copy(POOL, r_small[:, 24:25], r_F[:, 511:512], ["sc_sb0"], ["FMc"])
                copy(POOL, r_small[:, 25:26], r_M[:, 511:512], ["sc_sb1"], ["FMc"])
                chk('gates')
                for which, nm, nm2 in ((0, "mq", "mv"), (1, "mk", "mo")):
                    wb, wn = wnext(nm + tag)
                    wb2, wn2 = wnext(nm2 + tag, prefetch=False)
                    for cc in range(4):
                        c = which * 4 + cc
                        pb = 4 + c % 4
                        projB(wb, wn, cc * 128, 512, 128, pb)
                        st = stage[c % 2]
                        cacc = caccs[c % 2]
                        can = "cacc%d" % (c % 2)
                        sn = "stage%d" % (c % 2)
                        copy(DVE, st[:, 29:32], carry[:, c, 0:3], ["carry%d" % c], [sn])
                        copy(ACT, st[:, 32:544], ps[pb][:], ["ps%d" % pb], [sn])
                        P.op(DVE, lambda e, st=st, c=c: e.tensor_scalar(out=cacc[:], in0=st[:, 29:541], scalar1=cw[:, l, c, 0:1], scalar2=None,
                                                                        op0=ALU.mult), reads=[sn, "cw"], writes=[can])
                        for j in range(1, 4):
                            P.op(DVE, lambda e, st=st, c=c, j=j: e.scalar_tensor_tensor(out=cacc[:], in0=st[:, 29 + j:541 + j], scalar=cw[:, l, c, j:j + 1],
                                                                                     in1=cacc[:], op0=ALU.mult, op1=ALU.add),
                                 reads=[sn, "cw", can], writes=[can])
                        copy(DVE, carry[:, c, 0:3], st[:, 541:544], [sn], ["carry%d" % c])
                        P.op(ACT, lambda e, st=st: e.activation(out=st[:, 32:544], in_=cacc[:], func=AF.Sigmoid), reads=[can], writes=[sn])
                        P.op(DVE, lambda e, c=c, st=st: e.tensor_tensor(out=ar(A_QK + c)[:, 0, :], in0=cacc[:], in1=st[:, 32:544], op=ALU.mult),
                             reads=[can, sn], writes=[arn(A_QK + c)])
                        tt = cc
                        pb2 = 2 + tt % 2
                        projA(wb2, wn2, 0, 512, 512, tt, pb2)
                        if which == 0:
                            copy(ACT if tt % 2 else DVE, AP(vtok, tt * 520, [[65, 8], [1, 64]]), AP(ps[pb2], 0, [[64, 8], [1, 64]]), ["ps%d" % pb2],
                                 ["vtok%d" % tt])
                        else:
                            P.op(ACT, lambda e, tt=tt, pb2=pb2: e.activation(out=og[:, tt, :], in_=ps[pb2][:], func=AF.Sigmoid), reads=["ps%d" % pb2],
                                 writes=["og%d" % tt])
                if l == 0 and g == 0:
                    dbg("qkT", ar(A_QK, 8), [128, 8, 512], [arn(A_QK + c) for c in range(8)])
                wb, wn = wnext("xq" + tag)
                for c in range(4):
                    pb = c % 2
                    projB(wb, wn, c * 128, 512, 128, pb)
                    copy(evac_eng(), ar(A_XQ + c)[:, 0, :], ps[pb][:], ["ps%d" % pb], [arn(A_XQ + c)])

                if l == 0 and g == 0:
                    dbg("utok", utok[:], [128, 4, 8], ["utok"])
                    dbg("ftok", ftok[:], [128, 4, 8], ["ftok"])
                    dbg("decay_bc", decay_bc[:], [128, 4, 4], ["decay_bc"])
                chk('proj')
                chk('proj%d_%d' % (l, g))
                def swa_S(k):
                    n, hg = k // 2, k % 2
                    qs = slice(n * 128, (n + 1) * 128)
                    b0 = hg * 4
                    for pr in range(2):
                        pb = b0 + pr
                        for hh_ in range(2):
                            h = hg * 4 + pr * 2 + hh_
                            for half in range(2):
                                slot = hh_ * 2 + half
                                ks = slice(n * 128 + half * 128, n * 128 + half * 128 + 128)
                                mm(ps[pb][:, slot * 128:(slot + 1) * 128], kz[:, hg, h % 2, ks],
                                   ar(A_Q + h // 2)[:, 0, qs], True, True, ["kdup", arn(A_Q + h // 2)], ["ps%d" % pb],
                                   mark=(slot == 3))
                        h0 = hg * 4 + pr * 2
                        sc = sc_sb[pr]
                        P.op(DVE, lambda e, pb=pb, sc=sc, h0=h0: e.scalar_tensor_tensor(out=sc[:], in0=ps[pb][:], scalar=0.125,
                                                                                      in1=AP(biasT, h0 * 256, [[1, 512]]), op0=ALU.mult, op1=ALU.add),
                             reads=["ps%d" % pb, "biasT"], writes=["sc_sb%d" % pr])
                        pc = 24 + hg * 2 + pr
                        P.op(ACT, lambda e, sc=sc, pc=pc: e.activation(out=ar(pc)[:, 0, :], in_=sc[:], func=AF.Exp), reads=["sc_sb%d" % pr],
                             writes=[arn(pc)])

                def swa_V(k):
                    n, hg = k // 2, k % 2
                    gb = g * 4 + n
                    po_b = hg * 4 + 2
                    halves = [1] if gb == 0 else [0, 1]
                    for hl in range(4):
                        pr, hh_ = hl // 2, hl % 2
                        pc = 24 + hg * 2 + pr
                        for i_, half in enumerate(halves):
                            slot = hh_ * 2 + half
                            first, last = i_ == 0, i_ == len(halves) - 1
                            mm(ps[po_b][:, hl * 65:(hl + 1) * 65], ar(pc)[:, 0, slot * 128:(slot + 1) * 128], vswa[:, n + half, hg, :],
                               first, last, ["vswa", arn(pc)], ["ps%d" % po_b], mark=(last and hl == 3))
                    P.op(DVE, lambda e, po_b=po_b, hg=hg: e.tensor_tensor(out=AP(swt, hg * 4, [[1, 4], [1, 1]]), in0=AP(ps[po_b], 64, [[65, 4], [1, 1]]),
                                                                         in1=AP(sinkexp, l * 8 + hg * 4, [[1, 4], [1, 1]]), op=ALU.add),
                         reads=["ps%d" % po_b, "sinkexp"], writes=["swt%d" % hg])
                    P.op(DVE, lambda e, hg=hg: e.reciprocal(out=swt[:, hg * 4:hg * 4 + 4], in_=swt[:, hg * 4:hg * 4 + 4]), reads=["swt%d" % hg],
                         writes=["swt%d" % hg])
                    P.op(DVE, lambda e, po_b=po_b, hg=hg: e.tensor_tensor(out=AP(ysw_tok, hg * 256, [[64, 4], [1, 64]]),
                                                                         in0=AP(ps[po_b], 0, [[65, 4], [1, 64]]),
                                                                         in1=AP(swt, hg * 4, [[1, 4], [0, 64]]), op=ALU.mult),
                         reads=["ps%d" % po_b, "swt%d" % hg], writes=["ysw_tok%d" % hg])

                def swa_T(n):
                    tb = 3 if n % 2 == 0 else 7
                    for jc in range(4):
                        mm(ps[tb][:, jc * 128:(jc + 1) * 128], ysw_tok[:, jc * 128:(jc + 1) * 128], ident_bf[:], True, True,
                           ["ysw_tok%d" % (jc // 2), "ident_bf"], ["ps%d" % tb], mark=(jc == 3))
                    copy(ACT, AP(arena, A_YS * 512 + n * 128, [[512, 4], [1, 128]]), AP(ps[tb], 0, [[128, 4], [1, 128]]), ["ps%d" % tb],
                         [arn(A_YS + i_) for i_ in range(4)])

                swa_S(0)
                for k in range(8):
                    if k + 1 < 8:
                        swa_S(k + 1)
                    swa_V(k)
                    if k % 2 == 1:
                        swa_T(k // 2)
                copy(POOL, kz[:, :, :, 0:128], kz[:, :, :, 512:640], ["kdup"], ["kdup"])
                copy(POOL, vswa[:, 0, :, :], vswa[:, 4, :, :], ["vswa"], ["vswa"])
                if l == 0 and g == 0:
                    dbg("y_swaT", ar(A_YS, 4), [128, 4, 512], [arn(A_YS + i_) for i_ in range(4)])

                chk('swa')
                def xa_S(hx):
                    b0 = (hx % 2) * 4
                    for half in range(2):
                        pc = 8 + (hx % 2) * 2 + half
                        mm(ps[b0 + half][:], kxT[:, l, hx, half * 128:(half + 1) * 128], ar(A_XQ + hx)[:, 0, :], True, True,
                           ["kxT", arn(A_XQ + hx)], ["ps%d" % (b0 + half)])
                        P.op(ACT, lambda e, b0=b0, half=half, pc=pc: e.activation(out=ar(pc)[:, 0, :], in_=ps[b0 + half][:], func=AF.Exp,
                                                                                scale=128.0 ** -0.5), reads=["ps%d" % (b0 + half)], writes=[arn(pc)])

                def xa_V(hx):
                    b0 = (hx % 2) * 4
                    for tt in range(4):
                        tsl = slice(tt * 128, (tt + 1) * 128)
                        for half in range(2):
                            pc = 8 + (hx % 2) * 2 + half
                            mm(ps[b0 + 2][:, tsl], ar(pc)[:, 0, tsl], vx[:, l, half, hx * 128:(hx + 1) * 128], half == 0, half == 1, ["vx", arn(pc)],
                               ["ps%d" % (b0 + 2)], mark=(half == 1 and tt == 3))
                    for tt in range(4):
                        tsl = slice(tt * 128, (tt + 1) * 128)
                        for half in range(2):
                            pc = 8 + (hx % 2) * 2 + half
                            mm(ps[b0 + 3][:, tt:tt + 1], ar(pc)[:, 0, tsl], ones_bf[:, 0:1], half == 0, half == 1, ["ones_bf", arn(pc)],
                               ["ps%d" % (b0 + 3)], mark=(half == 1 and tt == 3))
                    P.op(DVE, lambda e, b0=b0: e.reciprocal(out=swt[:, 0:4], in_=ps[b0 + 3][:, 0:4]), reads=["ps%d" % (b0 + 3)], writes=["swt0"])
                    P.op(DVE, lambda e, b0=b0: e.tensor_tensor(out=yx_tok[:], in0=AP(ps[b0 + 2], 0, [[128, 4], [1, 128]]),
                                                              in1=AP(swt, 0, [[1, 4], [0, 128]]), op=ALU.mult),
                         reads=["ps%d" % (b0 + 2), "swt0"], writes=["yx_tok"])
                    for tt in range(4):
                        mm(ps[b0 + 3][:, tt * 128:(tt + 1) * 128], yx_tok[:, tt, :], ident_bf[:], True, True, ["yx_tok", "ident_bf"], ["ps%d" % (b0 + 3)],
                           mark=(tt == 3))
                    copy(ACT, ar(A_YX + hx)[:, 0, :], ps[b0 + 3][:], ["ps%d" % (b0 + 3)], [arn(A_YX + hx)])

                xa_S(0)
                for hx in range(4):
                    if hx + 1 < 4:
                        xa_S(hx + 1)
                    xa_V(hx)
                if l == 0 and g == 0:
                    dbg("y_xT", ar(A_YX, 4), [128, 4, 512], [arn(A_YX + i_) for i_ in range(4)])

                chk('xattn')
                for cc in range(4):
                    for par in range(2):
                        Z = 8 + 2 * cc + par
                        P.op(POOL, lambda e, Z=Z, par=par: e.memset(ar(Z, 1, (1 - par) * 64, 64)[:, 0, :], 0.0), writes=[arn(Z)])
                        copy(ACT if par else DVE, ar(Z, 1, par * 64, 64)[:, 0, :], ar(A_QK + 4 + cc, 1, par * 64, 64)[:, 0, :],
                             [arn(A_QK + 4 + cc)], [arn(Z)])
                def ml_front(tt):
                    ts_ = slice(tt * 128, (tt + 1) * 128)
                    for cc in range(4):
                        mm(ps[7][:, cc * 128:(cc + 1) * 128], ar(A_QK + 4 + cc)[:, 0, ts_], ident_bf[:], True, True,
                           [arn(A_QK + 4 + cc), "ident_bf"], ["ps7"], mark=(cc == 3))
                    copy(ACT, ktok[:], ps[7][:], ["ps7"], ["ktok"])
                    for h in range(8):
                        pb = h // 4
                        hl = h % 4
                        mm(ps[pb][:, hl * 128:(hl + 1) * 128], ar(8 + 2 * (h // 2) + h % 2)[:, 0, ts_], ar(A_QK + h // 2)[:, 0, ts_],
                           True, True, [arn(8 + 2 * (h // 2) + h % 2), arn(A_QK + h // 2)], ["ps%d" % pb], mark=(hl == 3))
                    P.op(DVE, lambda e, tt=tt: e.tensor_tensor(out=Cd32[:], in0=Cst[:], in1=AP(decay_bc, tt * 4, [[1, 4], [0, 65]]), op=ALU.mult),
                         reads=["Cst", "decay_bc"], writes=["Cd32"])
                    copy(ACT, AP(Cd16z, 0, [[130, 4], [1, 65]], 0, 64), AP(Cd32, 0, [[65, 4], [1, 65]], 0, 64), ["Cd32"], ["Cd16"])
                    copy(ACT, AP(Cd16z, 65, [[130, 4], [1, 65]], 64, 64), AP(Cd32, 0, [[65, 4], [1, 65]], 64, 64), ["Cd32"], ["Cd16"])
                    for pb in range(2):
                        P.op(DVE, lambda e, pb=pb: e.scalar_tensor_tensor(out=AP(STm, pb * 512, [[128, 4], [1, 128]]),
                                                                         in0=AP(ps[pb], 0, [[128, 4], [1, 128]]), scalar=0.125,
                                                                         in1=AP(mask01, 0, [[0, 4], [1, 128]]), op0=ALU.mult, op1=ALU.mult),
                             reads=["ps%d" % pb, "mask01"], writes=["STm%d" % pb])
                    P.op(POOL, lambda e, tt=tt: e.tensor_tensor(out=vu[:], in0=vtok[:, tt, :, :], in1=AP(utok, tt * 8, [[1, 8], [0, 65]]), op=ALU.mult),
                         reads=["vtok%d" % tt, "utok"], writes=["vu"])

                def ml_mid(tt):
                    ts_ = slice(tt * 128, (tt + 1) * 128)
                    for h in range(8):
                        pb = 2 + h // 4
                        hl = h % 4
                        mm(ps[pb][:, hl * 65:(hl + 1) * 65], STm[:, h, :], vu[:, h, :], True, False, ["STm%d" % (h // 4), "vu"], ["ps%d" % pb], mark=False)
                        mm(ps[pb][:, hl * 65:(hl + 1) * 65], ar(A_QK + h // 2)[:, 0, ts_], Cd16z[:, h, :], False, True,
                           [arn(A_QK + h // 2), "Cd16"], ["ps%d" % pb], mark=(hl == 3))
                    for h in range(8):
                        pb = 4 + h // 4
                        hl = h % 4
                        j = h // 2
                        mm(ps[pb][:, hl * 65:(hl + 1) * 65], ktok[:, j * 128:(j + 1) * 128], vu[:, h, :], True, True, ["ktok", "vu"],
                           ["ps%d" % pb], mark=(hl == 3))

                def ml_state(tt):
                    for pb in range(2):
                        for par in range(2):
                            P.op(DVE, lambda e, pb=pb, par=par: e.scalar_tensor_tensor(
                                out=AP(Cst, pb * 130, [[65, 2], [1, 65]], par * 64, 64),
                                in0=AP(ps[4 + pb], par * 65, [[130, 2], [1, 65]], par * 64, 64), scalar=0.125,
                                in1=AP(Cd32, pb * 130, [[65, 2], [1, 65]], par * 64, 64), op0=ALU.mult, op1=ALU.add),
                                reads=["ps%d" % (4 + pb), "Cd32"], writes=["Cst"])

                def ml_epi(tt):
                    for pb in range(2):
                        P.op(ACT, lambda e, pb=pb: e.activation(out=AP(lnt, pb * 4, [[1, 4], [1, 1]]), in_=AP(ps[2 + pb], 64, [[65, 4], [1, 1]]),
                                                               func=AF.Abs), reads=["ps%d" % (2 + pb)], writes=["lnt0"])
                    P.op(DVE, lambda e, tt=tt: e.tensor_tensor(out=lnt[:, 0, :], in0=lnt[:, 0, :], in1=ftok[:, tt, :], op=ALU.max),
                         reads=["lnt0", "ftok"], writes=["lnt0"])
                    P.op(DVE, lambda e: e.reciprocal(out=lnt[:, 0, :], in_=lnt[:, 0, :]), reads=["lnt0"], writes=["lnt0"])
                    for pb in range(2):
                        P.op(DVE, lambda e, pb=pb: e.tensor_tensor(out=hh[:, pb * 4:pb * 4 + 4, :], in0=AP(ps[2 + pb], 0, [[65, 4], [1, 64]]),
                                                                   in1=AP(lnt, pb * 4, [[1, 4], [0, 64]]), op=ALU.mult),
                             reads=["ps%d" % (2 + pb), "lnt0"], writes=["hh"])
                    if l == 0 and g == 0 and tt == 1:
                        dbg("hh", hh[:], [128, 8, 64], ["hh"])
                    P.op(DVE, lambda e: e.tensor_reduce(out=lnt[:, 1, :], in_=hh[:], axis=AX.X, op=ALU.add), reads=["hh"], writes=["lnt1"])
                    P.op(POOL, lambda e: e.tensor_tensor(out=sqh_ap, in0=hh[:], in1=hh[:], op=ALU.mult), reads=["hh"], writes=["sc_sb0"])
                    P.op(DVE, lambda e: e.tensor_reduce(out=lnt[:, 2, :], in_=sqh_ap, axis=AX.X, op=ALU.add), reads=["sc_sb0"], writes=["lnt2"])
                    P.op(DVE, lambda e: e.tensor_scalar(out=lnt[:, 1, :], in0=lnt[:, 1, :], scalar1=1.0 / 64, scalar2=None, op0=ALU.mult),
                         reads=["lnt1"], writes=["lnt1"])
                    P.op(DVE, lambda e: e.tensor_tensor(out=lnt[:, 3, :], in0=lnt[:, 1, :], in1=lnt[:, 1, :], op=ALU.mult),
                         reads=["lnt1"], writes=["lnt3"])
                    P.op(DVE, lambda e: e.scalar_tensor_tensor(out=lnt[:, 2, :], in0=lnt[:, 2, :], scalar=1.0 / 64, in1=lnt[:, 3, :],
                                                               op0=ALU.mult, op1=ALU.subtract), reads=["lnt2", "lnt3"], writes=["lnt2"])
                    P.op(DVE, lambda e: e.tensor_scalar(out=lnt[:, 2, :], in0=lnt[:, 2, :], scalar1=EPS, scalar2=None, op0=ALU.add),
                         reads=["lnt2"], writes=["lnt2"])
                    P.op(ACT, lambda e: e.activation(out=lnt[:, 2, :], in_=lnt[:, 2, :], func=AF.Sqrt), reads=["lnt2"], writes=["lnt2"])
                    P.op(DVE, lambda e: e.reciprocal(out=lnt[:, 2, :], in_=lnt[:, 2, :]), reads=["lnt2"], writes=["lnt2"])
                    P.op(DVE, lambda e: e.tensor_tensor(out=hh[:], in0=hh[:], in1=AP(lnt, 8, [[1, 8], [0, 64]]), op=ALU.subtract),
                         reads=["hh", "lnt1"], writes=["hh"])
                    P.op(DVE, lambda e: e.tensor_tensor(out=hh[:], in0=hh[:], in1=AP(lnt, 16, [[1, 8], [0, 64]]), op=ALU.mult),
                         reads=["hh", "lnt2"], writes=["hh"])
                    P.op(POOL, lambda e: e.tensor_tensor(out=AP(hh, 0, [[1, 512]]), in0=AP(hh, 0, [[1, 512]]), in1=normg_bc[:], op=ALU.mult),
                         reads=["hh", "normg"], writes=["hh"])
                    P.op(DVE, lambda e, tt=tt: e.tensor_tensor(out=ytok[:], in0=AP(hh, 0, [[1, 512]]), in1=og[:, tt, :], op=ALU.mult),
                         reads=["hh", "og%d" % tt], writes=["ytok"])

                def ml_tr(tt):
                    for jc in range(4):
                        mm(ps[6][:, jc * 128:(jc + 1) * 128], ytok[:, jc * 128:(jc + 1) * 128], ident_bf[:], True, True, ["ytok", "ident_bf"], ["ps6"],
                           mark=(jc == 3))
                    copy(ACT, AP(arena, A_YM * 512 + tt * 128, [[512, 4], [1, 128]]), AP(ps[6], 0, [[128, 4], [1, 128]]), ["ps6"], [arn(A_YM + i_) for i_ in range(4)])

                ml_front(0)
                for tt in range(4):
                    ml_mid(tt)
                    if tt > 0:
                        ml_tr(tt - 1)
                    ml_state(tt)
                    if tt + 1 < 4:
                        ml_front(tt + 1)
                    ml_epi(tt)
                if l == 0 and g == 0:
                    dbg("y_mlT", ar(A_YM, 4), [128, 4, 512], [arn(A_YM + i_) for i_ in range(4)])

                chk('mlstm')
                for j in range(8):
                    wbg, wng = wnext("g%d_" % j + tag)
                    bk = [(j * 6 + i_) % 8 for i_ in range(6)]
                    for b in range(3):
                        for kc in range(8):
                            mm(ps[bk[b]][:], AP(wbg, kc * 384 + b * 128, [[1, 128]]), hT[:, kc, :], kc == 0, kc == 7, [wng] + hres, ["ps%d" % bk[b]])
                        P.op(ACT, lambda e, b=b, bk=bk: e.activation(out=sg[:, b, :], in_=ps[bk[b]][:], func=AF.Sigmoid), reads=["ps%d" % bk[b]],
                             writes=["sg%d" % b])
                    if j == 0:
                        ml_tr(3)
                    wbb, wnb = wnext("br%d_" % j + tag)
                    for c in range(4):
                        mm(ps[bk[3]][:], AP(wbb, c * 128, [[1, 128]]), ar(A_YS + c)[:, 0, :], c == 0, c == 3, [wnb, arn(A_YS + c)],
                           ["ps%d" % bk[3]])
                    for c in range(4):
                        mm(ps[bk[4]][:], AP(wbb, 512 + c * 128, [[1, 128]]), ar(A_YM + c)[:, 0, :], c == 0, c == 3, [wnb, arn(A_YM + c)], ["ps%d" % bk[4]])
                    for c in range(4):
                        mm(ps[bk[5]][:], AP(wbb, 1024 + c * 128, [[1, 128]]), ar(A_YX + c)[:, 0, :], c == 0, c == 3, [wnb, arn(A_YX + c)], ["ps%d" % bk[5]])
                    P.op(DVE, lambda e, bk=bk: e.tensor_tensor(out=mtmp[0][:], in0=ps[bk[3]][:], in1=sg[:, 0, :], op=ALU.mult),
                         reads=["ps%d" % bk[3], "sg0"], writes=["mtmp0"])
                    P.op(DVE, lambda e, bk=bk: e.tensor_tensor(out=mtmp[1][:], in0=ps[bk[4]][:], in1=sg[:, 1, :], op=ALU.mult),
                         reads=["ps%d" % bk[4], "sg1"], writes=["mtmp1"])
                    P.op(DVE, lambda e: e.tensor_tensor(out=mtmp[0][:], in0=mtmp[0][:], in1=mtmp[1][:], op=ALU.add),
                         reads=["mtmp0", "mtmp1"], writes=["mtmp0"])
                    P.op(DVE, lambda e, bk=bk: e.tensor_tensor(out=mtmp[1][:], in0=ps[bk[5]][:], in1=sg[:, 2, :], op=ALU.mult),
                         reads=["ps%d" % bk[5], "sg2"], writes=["mtmp1"])
                    P.op(DVE, lambda e, j=j: e.tensor_tensor(out=ar(A_MG + j)[:, 0, :], in0=mtmp[0][:], in1=mtmp[1][:], op=ALU.add),
                         reads=["mtmp0", "mtmp1"], writes=[arn(A_MG + j)])
                if l == 0 and g == 0:
                    dbg("mergedT", ar(A_MG, 8), [128, 8, 512], [arn(A_MG + i_) for i_ in range(8)])
                chk('merge')
                for i in range(2):
                    wb, wn = wnext("wo%d_" % i + tag)
                    for dcl in range(4):
                        dc = i * 4 + dcl
                        pb = dc % 4
                        for kc in range(8):
                            mm(ps[pb][:], AP(wb, kc * 512 + dcl * 128, [[1, 128]]), ar(A_MG + kc)[:, 0, :], kc == 0, kc == 7, [wn, arn(A_MG + kc)],
                               ["ps%d" % pb])
                        P.op(DVE, lambda e, dc=dc, pb=pb: e.tensor_tensor(out=xT[:, dc, t0:t0 + TG], in0=ps[pb][:], in1=xT[:, dc, t0:t0 + TG], op=ALU.add),
                             reads=["ps%d" % pb, xres(dc, g)], writes=[xres(dc, g)])
                        if dc >= 2:
                            norm_sq(dc - 2, xT[:, dc - 2, t0:t0 + TG], xres(dc - 2, g), 4, TG)
                for dc in (6, 7):
                    norm_sq(dc, xT[:, dc, t0:t0 + TG], xres(dc, g), 4, TG)
                chk('outproj')
                norm_fin([xT[:, c, t0:t0 + TG] for c in range(8)], [xres(c, g) for c in range(8)], 2 + l, [hTb[:, c, :] for c in range(8)],
                         ["hTb%d" % c for c in range(8)], TG, 4)
                hbres = ["hTb%d" % c for c in range(8)]
                for i in range(8):
                    wb, wn = wnext("f1%d_" % i + tag)
                    if i == 0:
                        for kc in range(8):
                            for fl in range(4):
                                mm(ps[fl][:], AP(wb, kc * 512 + fl * 128, [[1, 128]]), hTb[:, kc, :], kc == 0, kc == 7, [wn, "hTb%d" % kc], ["ps%d" % fl])
                    for fl in range(4):
                        f = i * 4 + fl
                        pb = f % 4
                        for kc in range(8):
                            if i == 0:
                                break
                            mm(ps[pb][:], AP(wb, kc * 512 + fl * 128, [[1, 128]]), hTb[:, kc, :], kc == 0, kc == 7, [wn] + hbres, ["ps%d" % pb])
                        rb = r32[f % 2]
                        P.op(ACT, lambda e, pb=pb, rb=rb: e.activation(out=rb[:], in_=ps[pb][:], func=AF.Relu), reads=["ps%d" % pb], writes=["mtmp%d" % (f % 2)])
                        eng = DVE if f % 2 == 0 else POOL
                        P.op(eng, lambda e, rb=rb, f=f: e.tensor_tensor(out=ar(f)[:, 0, :], in0=rb[:], in1=rb[:], op=ALU.mult), reads=["mtmp%d" % (f % 2)],
                             writes=[arn(f)])
                for dc in range(8):
                    wb, wn = wnext("f2%d_" % dc + tag)
                    pb = 4 + dc % 4
                    for f in range(32):
                        mm(ps[pb][:], AP(wb, f * 128, [[1, 128]]), ar(f)[:, 0, :], f == 0, f == 31, [wn, arn(f)], ["ps%d" % pb])
                    P.op(DVE, lambda e, dc=dc, pb=pb: e.tensor_tensor(out=xT[:, dc, t0:t0 + TG], in0=ps[pb][:], in1=xT[:, dc, t0:t0 + TG], op=ALU.add),
                         reads=["ps%d" % pb, xres(dc, g)], writes=[xres(dc, g)])
                    if dc == 0:
                        if g + 1 < NTG:
                            mixer_norm(l, g + 1, 0)
                        elif l + 1 < DEPTH:
                            mixer_norm(l + 1, 0, 0)
                chk('tgend%d_%d' % (l, g))
                if l == 0 and g == 0:
                    dbg("xT0", xT[:, :, 0:TG], [128, 8, TG], [xres(c, 0) for c in range(8)])

        P.barrier()
        dma(SP, gfin_bc[:], wAP(gfin_d, 0, [[0, 128], [1, 1024]]), ["gfin_bc"])
        for i in range(T // 128):
            g = i // 4
            xf = xs_f[i % 2]
            xfn = "xs_f%d" % (i % 2)
            for half in range(2):
                pb = 1 + half + 2 * (i % 2)
                for j in range(4):
                    c = half * 4 + j
                    P.op(PE, lambda e, pb=pb, j=j, c=c, i=i: e.transpose(ps[pb][:, j * 128:(j + 1) * 128], xT[:, c, i * 128:(i + 1) * 128], ident[:]),
                         reads=[xres(c, g), "ident"], writes=["ps%d" % pb], mark=(j == 3))
                copy(evac_eng(), xf[:, half * 512:(half + 1) * 512], ps[pb][:], ["ps%d" % pb], [xfn])
            P.op(ACT, lambda e, xf=xf, i=i: e.activation(out=fsq[:], in_=xf[:], func=AF.Square, accum_out=fss[:, i:i + 1]), reads=[xfn],
                 writes=["fsq", "fss%d" % i])
            P.op(ACT, lambda e, i=i: e.activation(out=fss[:, i:i + 1], in_=fss[:, i:i + 1], func=AF.Ln, scale=1.0 / D, bias=epscol[:, 0:1]),
                 reads=["fss%d" % i, "epscol"], writes=["fss%d" % i])
            P.op(ACT, lambda e, i=i: e.activation(out=fss[:, i:i + 1], in_=fss[:, i:i + 1], func=AF.Exp, scale=-0.5), reads=["fss%d" % i],
                 writes=["fss%d" % i])
            P.op(DVE, lambda e, xf=xf, i=i: e.scalar_tensor_tensor(out=xf[:], in0=xf[:], scalar=fss[:, i:i + 1], in1=gfin_bc[:], op0=ALU.mult, op1=ALU.mult),
                 reads=[xfn, "fss%d" % i, "gfin_bc"], writes=[xfn])
            out_toks.append(dma(SP, out_d.ap()[i * 128:(i + 1) * 128, :], xf[:], ["out%d" % i], reads=[xfn]))
    except _Stop:
        stopped = True
    for name in dbg_out:
        out_toks.append(P.res["dbg_" + name][0])
    P.final_wait(SP, out_toks)
    assert stopped or wstate["used"] == len(wq), (wstate, len(wq))
    P.emit()
    return nc, dbg_out


_CACHE = {}


def kernel(**inputs):
    if "nc" not in _CACHE:
        _CACHE["nc"] = build()[0]
    nc = _CACHE["nc"]
    consts = host_consts()
    names = ["rel_bias", "g_mix", "w_in", "conv_w", "b_i", "b_f", "mlstm_norm_g", "sinks", "g_mem", "w_mem_kv", "w_br_swa",
             "w_br_mlstm", "w_br_x", "w_out", "g_ffn", "w_ff1", "w_ff2", "g_final"]
    shared = {k: np.ascontiguousarray(np.asarray(inputs[k], dtype=np.float32)) for k in names}
    x = np.asarray(inputs["x"], dtype=np.float32)
    mem = np.asarray(inputs["mem"], dtype=np.float32)
    in_maps = []
    for b in range(8):
        m = dict(shared)
        m.update(consts)
        m["x"] = np.ascontiguousarray(x[b])
        m["mem"] = np.ascontiguousarray(mem[b])
        in_maps.append(m)
    res = run_bass_kernel_spmd(nc, in_maps, core_ids=list(range(8)))
    return np.stack([np.asarray(res.results[b]["out"], dtype=np.float32) for b in range(8)], axis=0)
```

```python
import math
import os
import types
import numpy as np
import concourse.bass as bass
import concourse.mybir as mybir
from concourse.bass_utils import run_bass_kernel_spmd

F32 = mybir.dt.float32
BF16 = mybir.dt.bfloat16
AF = mybir.ActivationFunctionType
ALU = mybir.AluOpType
AX = mybir.AxisListType

PE, ACT, DVE, POOL, SP = "pe", "act", "dve", "pool", "sp"
COMPUTE = (PE, ACT, DVE, POOL)

D = 1024
T = 2048
DEPTH = 2
MEM = 256
NCOL = 6416
TG = 512
NTG = T // TG
EPS = 1e-6
NEG = -30000.0
C_SQ, C_SK, C_SV, C_MQ, C_MK, C_MV, C_MI, C_MF, C_MO, C_XQ, C_G = 0, 512, 640, 768, 1280, 1792, 2304, 2312, 2320, 2832, 3344


def _freeze(fn):
    if fn.__closure__ is None:
        return fn
    cells = []
    for c in fn.__closure__:
        try:
            cells.append(types.CellType(c.cell_contents))
        except ValueError:
            cells.append(c)
    return types.FunctionType(fn.__code__, fn.__globals__, fn.__name__, fn.__defaults__, tuple(cells))


class Prog:
    def __init__(self, nc):
        self.nc = nc
        self.q = {e: [] for e in (PE, ACT, DVE, POOL, SP)}
        self.cnt = {e: 0 for e in COMPUTE}
        self.sems = {e: nc.alloc_semaphore("s_" + e) for e in COMPUTE}
        self.dsem = {}
        self.waited = {e: {} for e in self.q}
        self.res = {}
        self.nwaits = 0

    def _need(self, e, tok, waits, raw):
        if tok is None:
            return
        src, sem, val = tok
        if src == e and src != "dma":
            if e == PE:
                return
        key = sem.num
        if self.waited[e].get(key, 0) >= val:
            return
        cur = waits.get(key)
        if cur is None or cur[1] < val:
            waits[key] = (sem, val)

    def op(self, e, fn, reads=(), writes=(), mark=True, dma_res=None, ndma=1):
        waits = {}
        for r in reads:
            st = self.res.get(r)
            if st is not None:
                self._need(e, st[0], waits, True)
        for w in writes:
            st = self.res.get(w)
            if st is not None:
                self._need(e, st[0], waits, False)
                for t in st[1]:
                    self._need(e, t, waits, False)
        wl = list(waits.values())
        for sem, val in wl:
            self.waited[e][sem.num] = val
        self.nwaits += len(wl)
        if dma_res is not None:
            sem, c = self.dsem.get(dma_res, (None, 0))
            if sem is None:
                sem = self.nc.alloc_semaphore("d%d" % len(self.dsem))
            c += 16 * ndma
            self.dsem[dma_res] = (sem, c)
            tok = ("dma", sem, c)
            inc = (sem, 16)
        elif mark:
            self.cnt[e] += 1
            tok = (e, self.sems[e], self.cnt[e])
            inc = (self.sems[e], 1)
        else:
            tok = (e, self.sems[e], self.cnt[e] + 1)
            inc = None
        for r in reads:
            st = self.res.setdefault(r, [None, []])
            st[1].append(tok)
            if len(st[1]) > 24:
                st[1] = self._compact(st[1])
        for w in writes:
            self.res[w] = [tok, []]
        self.q[e].append((wl, _freeze(fn), inc))
        return tok

    @staticmethod
    def _compact(toks):
        best = {}
        for t in toks:
            k = (t[0], t[1].num)
            if k not in best or best[k][2] < t[2]:
                best[k] = t
        return list(best.values())

    def barrier(self):
        toks = [(e, self.sems[e], self.cnt[e]) for e in COMPUTE if self.cnt[e] > 0]
        toks += [("dma", sem, c) for (sem, c) in self.dsem.values()]
        for e in self.q:
            waits = {}
            for t in toks:
                if t[0] == e:
                    continue
                self._need(e, t, waits, True)
            wl = list(waits.values())
            for sem, val in wl:
                self.waited[e][sem.num] = val
            self.q[e].append((wl, None, None))

    def final_wait(self, e, toks):
        waits = {}
        for t in toks:
            self._need(e, t, waits, True)
        self.q[e].append((list(waits.values()), None, None))

    def emit(self):
        nc = self.nc
        with nc.Block() as block:
            def run(e):
                def body(engine):
                    for wl, fn, inc in self.q[e]:
                        for sem, val in wl:
                            engine.wait_ge(sem, val)
                        if fn is None:
                            continue
                        ins = fn(engine)
                        if inc is not None:
                            if isinstance(ins, (list, tuple)):
                                for i_ in ins:
                                    i_.then_inc(inc[0], inc[1])
                            else:
                                ins.then_inc(inc[0], inc[1])
                return body
            block.tensor(run(PE))
            block.scalar(run(ACT))
            block.vector(run(DVE))
            block.gpsimd(run(POOL))
            block.sync(run(SP))


def t5_bucket(d):
    if d < 16:
        return d
    v = 16 + int(np.float32(np.log(np.float32(d) / np.float32(16)) * np.float32(16 / math.log(128 / 16))))
    return min(v, 31)


def host_consts():
    c = {}
    c["ident"] = np.eye(128, dtype=np.float32)
    c["jrev"] = np.eye(128, dtype=np.float32)[::-1].copy()
    oh = np.zeros((33, 384), np.float32)
    for j in range(384):
        d = j - 128
        if 0 <= d < 128:
            oh[t5_bucket(d), j] = 1.0
        else:
            oh[32, j] = NEG
    c["ohm"] = oh
    s = np.arange(128)[:, None]
    t = np.arange(128)[None, :]
    c["mask01"] = (s <= t).astype(np.float32)
    sel = np.zeros((8, 128), np.float32)
    for h in range(8):
        sel[h, (h % 2) * 64:(h % 2) * 64 + 64] = 1.0
    c["sel"] = sel
    pm = np.zeros((8, 4), np.float32)
    for h in range(8):
        pm[h, h // 2] = 1.0
    c["pm"] = pm
    return c


class _Stop(Exception):
    pass


def build(debug=(), stop=None):
    def chk(name):
        if stop == name:
            raise _Stop()
    nc = bass.Bass("TRN2", target_bir_lowering=False)
    P = Prog(nc)

    def din(name, shape):
        return nc.dram_tensor(name, list(shape), F32, kind="ExternalInput")

    x_d = din("x", [T, D]); mem_d = din("mem", [MEM, D]); relb_d = din("rel_bias", [32, 8])
    gmix_d = din("g_mix", [DEPTH, D]); win_d = din("w_in", [DEPTH, D, NCOL]); convw_d = din("conv_w", [DEPTH, 4, 1024])
    bi_d = din("b_i", [DEPTH, 8]); bf_d = din("b_f", [DEPTH, 8]); mng_d = din("mlstm_norm_g", [DEPTH, 512])
    sinks_d = din("sinks", [DEPTH, 8]); gmem_d = din("g_mem", [DEPTH, D]); wmkv_d = din("w_mem_kv", [DEPTH, D, 1024])
    wbs_d = din("w_br_swa", [DEPTH, 512, D]); wbm_d = din("w_br_mlstm", [DEPTH, 512, D]); wbx_d = din("w_br_x", [DEPTH, 512, D])
    wout_d = din("w_out", [DEPTH, D, D]); gffn_d = din("g_ffn", [DEPTH, D]); wff1_d = din("w_ff1", [DEPTH, D, 4096])
    wff2_d = din("w_ff2", [DEPTH, 4096, D]); gfin_d = din("g_final", [D])
    ident_d = din("ident", [128, 128]); jrev_d = din("jrev", [128, 128]); ohm_d = din("ohm", [33, 384])
    mask_d = din("mask01", [128, 128]); sel_d = din("sel", [8, 128]); pm_d = din("pm", [8, 4])
    out_d = nc.dram_tensor("out", [T, D], F32, kind="ExternalOutput")
    ext_d = nc.dram_tensor("ext_scratch", [8, 384], F32)
    dbg_out = {}

    alloc = {"off": int(nc.sbuf_base), "top": int(nc.sbuf_top)}

    def sb(name, shape, dt=F32):
        n = 1
        for s_ in shape[1:]:
            n *= s_
        nb = (n * mybir.dt.size(dt) + 31) // 32 * 32
        off = alloc["off"]
        assert off + nb <= alloc["top"], ("SBUF overflow", name, off, nb, alloc["top"])
        t = nc.alloc_sbuf_tensor_at(name, list(shape), dt, offset=off)
        alloc["off"] = off + nb
        return t

    def AP(t, off, dims, p0=0, npart=128):
        fs = 1
        for s_ in t.shape[1:]:
            fs *= s_
        return bass.AP(t, p0 * fs + off, [[fs, npart]] + [list(d_) for d_ in dims])

    alloc["off"] = (alloc["off"] + 31) // 32 * 32
    xT = sb("xT", [128, 8, T])
    wbuf = [sb("wbuf%d" % i, [128, 4096], BF16) for i in range(3)]
    biasT = sb("biasT", [128, 8, 2, 128])
    sc_sb = [sb("sc_sb%d" % i, [128, 512]) for i in range(2)]
    sqb = sc_sb
    rstd = sb("rstd", [128, 512])
    ident = sb("ident_sb", [128, 128]); ident_bf = sb("ident_bf", [128, 128], BF16)
    mask01 = sb("mask01_sb", [128, 128])
    ones_f = sb("ones_f", [128, 128]); ones_bf = sb("ones_bf", [128, 128], BF16)
    sel_sb = sb("sel_sb", [8, 128]); pm_sb = sb("pm_sb", [8, 4])
    epscol = sb("epscol", [128, 1])
    normg_bc = sb("normg_bc", [128, 512])
    gcols = sb("gcols", [128, 7, 8])
    cw = sb("cw", [128, DEPTH, 8, 4])
    bif = sb("bif", [8, 8])
    sinkexp = sb("sinkexp", [128, DEPTH, 8])
    kxT = sb("kxT", [128, DEPTH, 4, MEM], BF16); vx = sb("vx", [128, DEPTH, 2, 512], BF16)
    scope_mark = alloc["off"]
    jrev = sb("jrev_sb", [128, 128])
    lhs33 = sb("lhs33", [33, 8]); ohm_sb = sb("ohm_sb", [33, 384]); ext_sb = sb("ext_sb", [8, 384])
    hks = [sb("hk%d" % i, [128, 512]) for i in range(4)]
    xsb = [sb("xs%d" % i, [128, 1024]) for i in range(3)]
    memT = sb("memT", [128, 8, MEM]); memnT = sb("memnT", [128, 8, MEM], BF16)
    alloc["off"] = scope_mark
    xs_f = [sb("xs_f%d" % i, [128, 1024]) for i in range(2)]
    gfin_bc = sb("gfin_bc", [128, 1024])
    fsq = sb("fsq", [128, 1024])
    fss = sb("fss", [128, 16])
    alloc["off"] = scope_mark
    hT = sb("hT", [128, 8, TG], BF16)
    hTb = sb("hTb", [128, 8, TG], BF16)
    arena = sb("arena", [128, 32 * 512], BF16)
    kz = sb("kz", [128, 2, 2, 640], BF16)
    vswa = sb("vswa", [128, 5, 2, 65], BF16)
    ysw_tok = sb("ysw_tok", [128, 512], BF16)
    swt = sb("swt", [128, 8])
    yx_tok = sb("yx_tok", [128, 4, 128], BF16)
    stage = [sb("stage%d" % i, [128, 544]) for i in range(2)]
    carry = sb("carry", [128, 8, 4])
    caccs = [sb("cacc%d" % i, [128, 512]) for i in range(2)]
    ktok = sb("ktok", [128, 512], BF16)
    vtok = sb("vtok", [128, 4, 8, 65], BF16)
    og = sb("og", [128, 4, 512], BF16)
    sqh_ap = AP(sc_sb[0], 0, [[64, 8], [1, 64]])
    STm = sb("STm", [128, 8, 128], BF16)
    vu = sb("vu", [128, 8, 65], BF16)
    hh = sb("hh", [128, 8, 64])
    ytok = sb("ytok", [128, 512], BF16)
    lnt = sb("lnt", [128, 8, 8])
    Cst = sb("Cst", [128, 4, 65])
    Cd32 = sb("Cd32", [128, 4, 65])
    Cd16z = sb("Cd16z", [128, 8, 65], BF16)
    Rrows = sb("Rrows", [128, 512])
    r_small = sb("r_small", [8, 64])
    utok = sb("utok", [128, 4, 8]); ftok = sb("ftok", [128, 4, 8])
    decay_bc = sb("decay_bc", [128, 4, 4])
    sg = sb("sg", [128, 3, 512], BF16)
    mtmp = [sb("mtmp%d" % i, [128, 512]) for i in range(2)]
    r32 = mtmp
    r_ia = mtmp[0][0:8, :]; r_sp = mtmp[1][0:8, :]; r_F = sc_sb[0][0:8, :]; r_M = sc_sb[1][0:8, :]
    print("SBUF main scope end", alloc["off"], "top", alloc["top"])
    ps = [nc.alloc_psum_tensor("ps%d" % i, [128, 512], F32) for i in range(8)]

    out_toks = []
    stopped = False
    try:
        def ar(chunk, n=1, p0=0, npart=128, sub=None):
            return AP(arena, chunk * 512, [[512, n], [1, 512]], p0, npart)
        A_QK, A_Q, A_XQ, A_YS, A_MG, A_YM, A_YX = 0, 8, 12, 16, 0, 24, 28

        def arn(c):
            return "ar%d" % c

        def dma(e, out, in_, writes, reads=(), res=None, **kw):
            return P.op(e, lambda eng: eng.dma_start(out=out, in_=in_, **kw), reads=reads, writes=writes,
                        dma_res=res or writes[0])

        rr = {"ev": 0}

        def evac_eng():
            rr["ev"] ^= 1
            return ACT if rr["ev"] else DVE

        def copy(e, out, in_, reads, writes):
            if e == ACT:
                return P.op(ACT, lambda eng: eng.activation(out=out, in_=in_, func=AF.Copy), reads=reads, writes=writes)
            return P.op(e, lambda eng: eng.tensor_copy(out=out, in_=in_), reads=reads, writes=writes)

        def mm(out, lhsT, rhs, start, stop, reads, writes, mark=None):
            return P.op(PE, lambda eng: eng.matmul(out, lhsT, rhs, start=start, stop=stop), reads=reads, writes=writes,
                        mark=stop if mark is None else mark)

        wq = []
        wstate = {"issued": 0, "used": 0}

        def wissue():
            i = wstate["issued"]
            if i >= len(wq):
                return
            name, fn = wq[i]
            b = i % 3
            pairs = fn(wbuf[b])
            P.op(POOL, lambda eng: [eng.dma_start(out=o, in_=s_) for o, s_ in pairs], writes=["wbuf%d" % b],
                 dma_res="wbuf%d" % b, ndma=len(pairs))
            wstate["issued"] += 1

        def wnext(name, prefetch=True):
            i = wstate["used"]
            assert wq[i][0] == name, (wq[i][0], name)
            assert wstate["issued"] > i or prefetch
            while prefetch and wstate["issued"] < min(i + 3, len(wq)):
                wissue()
            wstate["used"] += 1
            return wbuf[i % 3], "wbuf%d" % (i % 3)

        def wAP(t, off, dims):
            return bass.AP(t, off, [list(d_) for d_ in dims])

        def blk_k1024(wd, base, rowlen, col0, ncols):
            def fn(buf):
                return [(AP(buf, 0, [[ncols, 8], [1, ncols]]),
                         wAP(wd, base + col0, [[rowlen, 128], [128 * rowlen, 8], [1, ncols]]))]
            return fn

        def blk_b1(l):
            base = l * D * NCOL

            def fn(buf):
                prs = []
                for kv in range(2):
                    for dup in range(2):
                        prs.append((AP(buf, kv * 128 + dup * 64, [[400, 8], [1, 64]]),
                                    wAP(win_d, base + C_SK + kv * 64, [[NCOL, 128], [128 * NCOL, 8], [1, 64]])))
                prs.append((AP(buf, 256, [[400, 8], [1, 128]]),
                            wAP(win_d, base + C_SV, [[NCOL, 128], [128 * NCOL, 8], [1, 128]])))
                prs.append((AP(buf, 384, [[400, 8], [1, 16]]),
                            wAP(win_d, base + C_MI, [[NCOL, 128], [128 * NCOL, 8], [1, 16]])))
                return prs
            return fn

        def blk_gate(l, j):
            base = l * D * NCOL

            def fn(buf):
                return [(AP(buf, b * 128, [[384, 8], [1, 128]]),
                         wAP(win_d, base + C_G + b * 1024 + j * 128, [[NCOL, 128], [128 * NCOL, 8], [1, 128]]))
                        for b in range(3)]
            return fn

        def blk_br(l, j):
            def fn(buf):
                prs = [(AP(buf, 0, [[128, 4], [1, 128]]),
                        wAP(wbs_d, l * 512 * D + j * 128, [[D, 128], [128 * D, 4], [1, 128]]))]
                prs.append((AP(buf, 512, [[128, 4], [1, 128]]),
                            wAP(wbm_d, l * 512 * D + j * 128, [[D, 128], [128 * D, 4], [1, 128]])))
                prs.append((AP(buf, 1024, [[128, 4], [1, 128]]),
                            wAP(wbx_d, l * 512 * D + j * 128, [[D, 128], [128 * D, 4], [1, 128]])))
                return prs
            return fn

        def blk_ff2(l, dc):
            def fn(buf):
                return [(AP(buf, 0, [[128, 32], [1, 128]]),
                         wAP(wff2_d, l * 4096 * D + dc * 128, [[D, 128], [128 * D, 32], [1, 128]]))]
            return fn

        for l in range(DEPTH):
            wq.append(("memk%d" % l, blk_k1024(wmkv_d, l * D * 1024, 1024, 0, 512)))
            wq.append(("memv%d" % l, blk_k1024(wmkv_d, l * D * 1024, 1024, 512, 512)))
        for l in range(DEPTH):
            for g in range(NTG):
                tag = "%d_%d" % (l, g)
                wq.append(("b1" + tag, blk_b1(l)))
                wq.append(("sq" + tag, blk_k1024(win_d, l * D * NCOL, NCOL, C_SQ, 512)))
                wq.append(("mq" + tag, blk_k1024(win_d, l * D * NCOL, NCOL, C_MQ, 512)))
                wq.append(("mv" + tag, blk_k1024(win_d, l * D * NCOL, NCOL, C_MV, 512)))
                wq.append(("mk" + tag, blk_k1024(win_d, l * D * NCOL, NCOL, C_MK, 512)))
                wq.append(("mo" + tag, blk_k1024(win_d, l * D * NCOL, NCOL, C_MO, 512)))
                wq.append(("xq" + tag, blk_k1024(win_d, l * D * NCOL, NCOL, C_XQ, 512)))
                for j in range(8):
                    wq.append(("g%d_" % j + tag, blk_gate(l, j)))
                    wq.append(("br%d_" % j + tag, blk_br(l, j)))
                for i in range(2):
                    wq.append(("wo%d_" % i + tag, blk_k1024(wout_d, l * D * D, D, i * 512, 512)))
                for i in range(8):
                    wq.append(("f1%d_" % i + tag, blk_k1024(wff1_d, l * D * 4096, 4096, i * 512, 512)))
                for i in range(8):
                    wq.append(("f2%d_" % i + tag, blk_ff2(l, i)))

        dma(SP, ident[:], ident_d.ap(), ["ident"])
        dma(POOL, ohm_sb[:], ohm_d.ap(), ["ohm"])
        P.op(DVE, lambda e: e.memset(lhs33[:], 1.0), writes=["lhs33"])
        dma(POOL, lhs33[0:32, :], relb_d.ap(), ["lhs33"], res="relb")
        dma(POOL, jrev[:], jrev_d.ap(), ["jrev"])
        dma(POOL, mask01[:], mask_d.ap(), ["mask01"])
        dma(POOL, sel_sb[:], sel_d.ap(), ["sel"])
        dma(POOL, pm_sb[:], pm_d.ap(), ["pm"])
        copy(DVE, ident_bf[:], ident[:], ["ident"], ["ident_bf"])
        P.op(DVE, lambda e: e.memset(ones_f[:], 1.0), writes=["ones_f"])
        P.op(DVE, lambda e: e.memset(epscol[:], EPS), writes=["epscol"])
        P.op(DVE, lambda e: e.memset(ones_bf[:], 1.0), writes=["ones_bf"])
        gsrcs = [(gmix_d, 0), (gmix_d, D), (gffn_d, 0), (gffn_d, D), (gmem_d, 0), (gmem_d, D), (gfin_d, 0)]
        for i, (gd, off) in enumerate(gsrcs):
            dma(ACT, gcols[:, i, :], wAP(gd, off, [[1, 128], [128, 8]]), ["gcols"], res="gcols%d" % i,
                allow_slow_non_contiguous=True)
        for l in range(DEPTH):
            dma(ACT, sinkexp[:, l, :], wAP(sinks_d, l * 8, [[0, 128], [1, 8]]), ["sinkraw"], res="sink%d" % l)
            dma(ACT, bif[:, l:l + 1], wAP(bi_d, l * 8, [[1, 8], [1, 1]]), ["bif"], res="bi%d" % l)
            dma(ACT, bif[:, 2 + l:3 + l], wAP(bf_d, l * 8, [[1, 8], [1, 1]]), ["bif"], res="bf%d" % l)
        P.op(ACT, lambda e: e.activation(out=sinkexp[:], in_=sinkexp[:], func=AF.Exp), reads=["sinkraw"], writes=["sinkexp"])
        P.op(DVE, lambda e: e.tensor_scalar(out=bif[:, 4:6], in0=bif[:, 2:4], scalar1=-1.0, scalar2=None, op0=ALU.mult),
             reads=["bif"], writes=["nbf"])
        mm(ps[0][0:8, 0:384], lhs33[:], ohm_sb[:], True, True, ["lhs33", "ohm"], ["ps0"])
        copy(DVE, ext_sb[:], ps[0][0:8, 0:384], ["ps0"], ["ext_sb"])
        dma(POOL, ext_d.ap(), ext_sb[:], ["ext_d"], reads=["ext_sb"])
        for half in range(2):
            for h4 in range(2):
                off = 129 if half == 0 else 1
                bi_ = half * 2 + h4
                for hl in range(4):
                    h = h4 * 4 + hl
                    dma(POOL, hks[bi_][:, hl * 128:(hl + 1) * 128], wAP(ext_d, h * 384 + off, [[1, 128], [1, 128]]), ["hk%d_%d" % (bi_, hl)],
                        reads=["ext_d"])

        chk('consts')
        for i in range(T // 128):
            xs = xsb[i % 3]
            xsn = "xs%d" % (i % 3)
            dma(SP, xs[:], x_d.ap()[i * 128:(i + 1) * 128, :], [xsn])
            for half in range(2):
                pb = ps[2 + (2 * i + half) % 4]
                pn = "ps%d" % (2 + (2 * i + half) % 4)
                for j in range(4):
                    c = half * 4 + j
                    P.op(PE, lambda e, pb=pb, j=j, c=c: e.transpose(pb[:, j * 128:(j + 1) * 128], xs[:, c * 128:(c + 1) * 128], ident[:]),
                         reads=[xsn, "ident"], writes=[pn], mark=(j == 3))
                copy(DVE, AP(xT, half * 4 * T + i * 128, [[T, 4], [1, 128]]), AP(pb, 0, [[128, 4], [1, 128]]),
                     [pn], ["xT%d_%d" % (c_, i // 4) for c_ in range(half * 4, half * 4 + 4)])
        for i in range(MEM // 128):
            xs = xsb[(i + 1) % 3]
            xsn = "xs%d" % ((i + 1) % 3)
            dma(SP, xs[:], mem_d.ap()[i * 128:(i + 1) * 128, :], [xsn])
            for half in range(2):
                pb = ps[2 + (2 * i + half) % 4]
                pn = "ps%d" % (2 + (2 * i + half) % 4)
                for j in range(4):
                    c = half * 4 + j
                    P.op(PE, lambda e, pb=pb, j=j, c=c: e.transpose(pb[:, j * 128:(j + 1) * 128], xs[:, c * 128:(c + 1) * 128], ident[:]),
                         reads=[xsn, "ident"], writes=[pn], mark=(j == 3))
                copy(DVE, AP(memT, half * 4 * MEM + i * 128, [[MEM, 4], [1, 128]]), AP(pb, 0, [[128, 4], [1, 128]]),
                     [pn], ["memT"])

        for l in range(DEPTH):
            for c in range(8):
                dma(SP, cw[:, l, c, :], wAP(convw_d, l * 4096 + c * 128, [[1, 128], [1024, 4]]), ["cw"],
                    res="cw%d_%d" % (l, c), allow_slow_non_contiguous=True)
        for half in range(2):
            for h4 in range(2):
                bi_ = half * 2 + h4
                mm(ps[1][:], jrev[:], hks[bi_][:], True, True, ["jrev"] + ["hk%d_%d" % (bi_, hl) for hl in range(4)], ["ps1"])
                copy(DVE, AP(biasT, h4 * 4 * 256 + half * 128, [[256, 4], [1, 128]]),
                     AP(ps[1], 0, [[128, 4], [1, 128]]), ["ps1"], ["biasT"])
        chk('load')
        def rmsnorm(src_fn, src_res_fn, gi, dst_fn, dst_res_fn, ntok, psb):
            pn = "ps%d" % psb
            srcs = [src_fn(c) for c in range(8)]
            dsts = [dst_fn(c) for c in range(8)]
            for c in range(8):
                sq = sqb[c % 2]
                P.op(ACT, lambda e, c=c, sq=sq: e.activation(out=sq[:, 0:ntok], in_=srcs[c], func=AF.Square),
                     reads=[src_res_fn(c)], writes=["sc_sb%d" % (c % 2)])
                mm(ps[psb][:, 0:ntok], ones_f[:], sq[:, 0:ntok], c == 0, c == 7, ["sc_sb%d" % (c % 2), "ones_f"], [pn], mark=True)
            P.op(ACT, lambda e: e.activation(out=rstd[:, 0:ntok], in_=ps[psb][:, 0:ntok], func=AF.Ln, scale=1.0 / D, bias=epscol[:, 0:1]),
                 reads=[pn, "epscol"], writes=["rstd"])
            P.op(ACT, lambda e: e.activation(out=rstd[:, 0:ntok], in_=rstd[:, 0:ntok], func=AF.Exp, scale=-0.5), reads=["rstd"], writes=["rstd"])
            for c in range(8):
                eng = DVE
                P.op(eng, lambda e, c=c: e.scalar_tensor_tensor(out=dsts[c], in0=srcs[c], scalar=gcols[:, gi, c:c + 1],
                                                                in1=rstd[:, 0:ntok], op0=ALU.mult, op1=ALU.mult),
                     reads=[src_res_fn(c), "rstd", "gcols"], writes=[dst_res_fn(c)])

        def norm_sq(c, src_ap, src_res, psb, ntok):
            sq = sqb[c % 2]
            P.op(ACT, lambda e: e.activation(out=sq[:, 0:ntok], in_=src_ap, func=AF.Square), reads=[src_res], writes=["sc_sb%d" % (c % 2)])
            mm(ps[psb][:, 0:ntok], ones_f[:], sq[:, 0:ntok], c == 0, c == 7, ["sc_sb%d" % (c % 2), "ones_f"], ["ps%d" % psb], mark=True)

        def norm_fin(srcs, src_res, gi, dsts, dst_res, ntok, psb):
            pn = "ps%d" % psb
            P.op(ACT, lambda e: e.activation(out=rstd[:, 0:ntok], in_=ps[psb][:, 0:ntok], func=AF.Ln, scale=1.0 / D, bias=epscol[:, 0:1]),
                 reads=[pn, "epscol"], writes=["rstd"])
            P.op(ACT, lambda e: e.activation(out=rstd[:, 0:ntok], in_=rstd[:, 0:ntok], func=AF.Exp, scale=-0.5), reads=["rstd"], writes=["rstd"])
            for c in range(8):
                P.op(DVE, lambda e, c=c: e.scalar_tensor_tensor(out=dsts[c], in0=srcs[c], scalar=gcols[:, gi, c:c + 1],
                                                                in1=rstd[:, 0:ntok], op0=ALU.mult, op1=ALU.mult),
                     reads=[src_res[c], "rstd", "gcols"], writes=[dst_res[c]])

        def mixer_norm(l_, g_, psb):
            t_ = g_ * TG
            rmsnorm(lambda c: xT[:, c, t_:t_ + TG], lambda c: xres(c, g_), l_, lambda c: hT[:, c, :], lambda c: "hT%d" % c, TG, psb)

        def xres(c, g):
            return "xT%d_%d" % (c, g)

        def dbg(name, ap, shape, reads):
            if name in debug:
                d = nc.dram_tensor("dbg_" + name, list(shape), ap.dtype, kind="ExternalOutput")
                dbg_out[name] = d
                dma(SP, d.ap(), ap, ["dbg_" + name], reads=reads)

        for l in range(DEPTH):
            rmsnorm(lambda c: memT[:, c, :], lambda c: "memT", 4 + l, lambda c: memnT[:, c, :], lambda c: "memnT", MEM, 0)
            wb, wn = wnext("memk%d" % l)
            for hx in range(4):
                pb = 1 + hx % 2
                for kc in range(8):
                    mm(ps[pb][:, 0:MEM], AP(wb, kc * 512 + hx * 128, [[1, 128]]), memnT[:, kc, :], kc == 0, kc == 7,
                       [wn, "memnT"], ["ps%d" % pb])
                copy(evac_eng(), kxT[:, l, hx, :], ps[pb][:, 0:MEM], ["ps%d" % pb], ["kxT"])
            wb, wn = wnext("memv%d" % l)
            for mt in range(2):
                pb = 3 + mt
                for kc in range(8):
                    mm(ps[pb][:], memnT[:, kc, mt * 128:(mt + 1) * 128], AP(wb, kc * 512, [[1, 512]]), kc == 0, kc == 7,
                       [wn, "memnT"], ["ps%d" % pb])
                copy(evac_eng(), vx[:, l, mt, :], ps[pb][:], ["ps%d" % pb], ["vx"])
        chk('memkv')
        P.barrier()
        P.op(DVE, lambda e: e.memset(Rrows[:], 0.0), writes=["Rrows"])
        P.op(POOL, lambda e: e.memset(vtok[:], 1.0), writes=["vtok0", "vtok1", "vtok2", "vtok3"])
        for l in range(DEPTH):
            dma(SP, normg_bc[:], wAP(mng_d, l * 512, [[0, 128], [1, 512]]), ["normg"])
            P.op(POOL, lambda e: e.memset(carry[:], 0.0), writes=["carry%d" % c for c in range(8)])
            P.op(POOL, lambda e: e.memset(kz[:], 0.0), writes=["kdup"])
            P.op(POOL, lambda e: e.memset(Cd16z[:], 0.0), writes=["Cd16"])
            P.op(POOL, lambda e: e.memset(vswa[:], 1.0), writes=["vswa"])
            P.op(POOL, lambda e: e.memset(Cst[:], 0.0), writes=["Cst"])
            P.op(POOL, lambda e: e.memset(r_small[:, 24:26], 0.0), writes=["FMc"])

            for g in range(NTG):
                tag = "%d_%d" % (l, g)
                t0 = g * TG
                if l == 0 and g == 0:
                    mixer_norm(0, 0, 0)
                hres = ["hT%d" % c for c in range(8)]
                if l == 0 and g == 0:
                    dbg("hT", hT[:], [128, 8, TG], hres)

                def projB(wb, wn, coloff, stride, ncols, pb, p_cols=TG):
                    for kc in range(8):
                        mm(ps[pb][0:ncols, 0:TG], AP(wb, kc * stride + coloff, [[1, ncols]]), hT[:, kc, :], kc == 0, kc == 7,
                           [wn] + hres, ["ps%d" % pb])

                def projA(wb, wn, coloff, stride, ncols, tt, pb):
                    for kc in range(8):
                        mm(ps[pb][:, 0:ncols], hT[:, kc, tt * 128:(tt + 1) * 128], AP(wb, kc * stride + coloff, [[1, ncols]]),
                           kc == 0, kc == 7, [wn] + hres, ["ps%d" % pb])

                if l == 0 and g == 0:
                    dbg('biasT', biasT[:], [128, 8, 2, 128], ['biasT'])
                chk('norm')
                wb, wn = wnext("b1" + tag)
                for kv in range(2):
                    pb = 4 + kv
                    projB(wb, wn, kv * 128, 400, 128, pb)
                    copy(ACT, kz[0:64, kv, 0, 128:640], ps[pb][0:64, :], ["ps%d" % pb], ["kdup"])
                    copy(DVE, kz[64:128, kv, 1, 128:640], ps[pb][64:128, :], ["ps%d" % pb], ["kdup"])
                for tt in range(4):
                    pb = 6 + tt % 2
                    projA(wb, wn, 256, 400, 128, tt, pb)
                    copy(evac_eng(), AP(vswa, (1 + tt) * 130, [[65, 2], [1, 64]]), AP(ps[pb], 0, [[64, 2], [1, 64]]), ["ps%d" % pb], ["vswa"])
                projB(wb, wn, 384, 400, 8, 0)
                projB(wb, wn, 392, 400, 8, 1)
                P.op(ACT, lambda e: e.activation(out=r_ia, in_=ps[0][0:8, :], func=AF.Identity, bias=bif[:, l:l + 1]),
                     reads=["ps0", "bif"], writes=["mtmp0"])
                P.op(ACT, lambda e: e.activation(out=r_sp, in_=ps[1][0:8, :], func=AF.Exp, bias=bif[:, 4 + l:5 + l], scale=-1.0),
                     reads=["ps1", "nbf"], writes=["mtmp1"])
                P.op(ACT, lambda e: e.activation(out=r_sp, in_=r_sp, func=AF.Ln, bias=ones_f[0:8, 0:1]), reads=["mtmp1", "ones_f"],
                     writes=["mtmp1"])
                P.op(DVE, lambda e: e.tensor_tensor_scan(out=r_F, data0=AP(ones_f, 0, [[0, 512]], 0, 8), data1=r_sp, initial=r_small[:, 24:25],
                                                         op0=ALU.mult, op1=ALU.add), reads=["mtmp1", "ones_f", "FMc"], writes=["sc_sb0"])
                P.op(DVE, lambda e: e.tensor_tensor(out=r_ia, in0=r_ia, in1=r_F, op=ALU.add),
                     reads=["mtmp0", "sc_sb0"], writes=["mtmp0"])
                P.op(DVE, lambda e: e.tensor_tensor_scan(out=r_M, data0=AP(ones_f, 0, [[0, 512]], 0, 8), data1=r_ia, initial=r_small[:, 25:26],
                                                         op0=ALU.mult, op1=ALU.max), reads=["mtmp0", "ones_f", "FMc"], writes=["sc_sb1"])
                for c in range(4):
                    P.op(DVE, lambda e, c=c: e.tensor_scalar(out=r_small[:, c:c + 1], in0=r_M[:, 128 * c + 127:128 * c + 128],
                                                             scalar1=-1.0, scalar2=None, op0=ALU.mult), reads=["sc_sb1"], writes=["nme"])
                for c in range(4):
                    cs = slice(c * 128, (c + 1) * 128)
                    P.op(ACT, lambda e, c=c, cs=cs: e.activation(out=Rrows[0:8, cs], in_=r_ia[:, cs], func=AF.Exp,
                                                                 bias=r_small[:, c:c + 1]), reads=["mtmp0", "nme"], writes=["Rrows"])
                    P.op(ACT, lambda e, c=c, cs=cs: e.activation(out=Rrows[32:40, cs], in_=r_F[:, c * 128:128 + c * 128], func=AF.Exp,
                                                                 bias=r_small[:, c:c + 1]), reads=["sc_sb0", "nme"], writes=["Rrows"])
                    min_ap = r_M[:, 128 * c - 1:128 * c] if c > 0 else r_small[:, 25:26]
                    P.op(ACT, lambda e, c=c, min_ap=min_ap: e.activation(out=r_small[:, 4 + c:5 + c], in_=min_ap, func=AF.Exp,
                                                                         bias=r_small[:, c:c + 1]), reads=["sc_sb1", "FMc", "nme"], writes=["decay"])
                    P.op(DVE, lambda e, c=c: e.tensor_scalar(out=r_small[:, 8 + 4 * c:12 + 4 * c], in0=pm_sb[:], scalar1=r_small[:, 4 + c:5 + c],
                                                             scalar2=None, op0=ALU.mult), reads=["decay", "pm"], writes=["Dg"])
                wb, wn = wnext("sq" + tag)
                for c in range(4):
                    pb = c % 4
                    projB(wb, wn, c * 128, 512, 128, pb)
                    copy(evac_eng(), ar(A_Q + c)[:, 0, :], ps[pb][:], ["ps%d" % pb], [arn(A_Q + c)])
                mm(ps[2][:, 0:16], sel_sb[:], r_small[:, 8:24], True, True, ["sel", "Dg"], ["ps2"])
                copy(DVE, decay_bc[:], AP(ps[2], 0, [[4, 4], [1, 4]]), ["ps2"], ["decay_bc"])
                for tt in range(4):
                    P.op(PE, lambda e, tt=tt: e.transpose(ps[3][:, 0:64], Rrows[0:64, tt * 128:(tt + 1) * 128], ident[0:64, 0:64]),
                         reads=["Rrows", "ident"], writes=["ps3"])
                    copy(DVE, utok[:, tt, :], ps[3][:, 0:8], ["ps3"], ["utok"])
                    copy(ACT, ftok[:, tt, :], ps[3][:, 32:40], ["ps3"], ["ftok"])
                copy(POOL, r_small[:, 24:25], r_F[:, 511:512], ["sc_sb0"], ["FMc"])
                copy(POOL, r_small[:, 25:26], r_M[:, 511:512], ["sc_sb1"], ["FMc"])
                chk('gates')
                for which, nm, nm2 in ((0, "mq", "mv"), (1, "mk", "mo")):
                    wb, wn = wnext(nm + tag)
                    wb2, wn2 = wnext(nm2 + tag, prefetch=False)
                    for cc in range(4):
                        c = which * 4 + cc
                        pb = 4 + c % 4
                        projB(wb, wn, cc * 128, 512, 128, pb)
                        st = stage[c % 2]
                        cacc = caccs[c % 2]
                        can = "cacc%d" % (c % 2)
                        sn = "stage%d" % (c % 2)
                        copy(DVE, st[:, 29:32], carry[:, c, 0:3], ["carry%d" % c], [sn])
                        copy(ACT, st[:, 32:544], ps[pb][:], ["ps%d" % pb], [sn])
                        P.op(DVE, lambda e, st=st, c=c: e.tensor_scalar(out=cacc[:], in0=st[:, 29:541], scalar1=cw[:, l, c, 0:1], scalar2=None,
                                                                        op0=ALU.mult), reads=[sn, "cw"], writes=[can])
                        for j in range(1, 4):
                            P.op(DVE, lambda e, st=st, c=c, j=j: e.scalar_tensor_tensor(out=cacc[:], in0=st[:, 29 + j:541 + j], scalar=cw[:, l, c, j:j + 1],
                                                                                     in1=cacc[:], op0=ALU.mult, op1=ALU.add),
                                 reads=[sn, "cw", can], writes=[can])
                        copy(DVE, carry[:, c, 0:3], st[:, 541:544], [sn], ["carry%d" % c])
                        P.op(ACT, lambda e, st=st: e.activation(out=st[:, 32:544], in_=cacc[:], func=AF.Sigmoid), reads=[can], writes=[sn])
                        P.op(DVE, lambda e, c=c, st=st: e.tensor_tensor(out=ar(A_QK + c)[:, 0, :], in0=cacc[:], in1=st[:, 32:544], op=ALU.mult),
                             reads=[can, sn], writes=[arn(A_QK + c)])
                        tt = cc
                        pb2 = 2 + tt % 2
                        projA(wb2, wn2, 0, 512, 512, tt, pb2)
                        if which == 0:
                            copy(ACT if tt % 2 else DVE, AP(vtok, tt * 520, [[65, 8], [1, 64]]), AP(ps[pb2], 0, [[64, 8], [1, 64]]), ["ps%d" % pb2],
                                 ["vtok%d" % tt])
                        else:
                            P.op(ACT, lambda e, tt=tt, pb2=pb2: e.activation(out=og[:, tt, :], in_=ps[pb2][:], func=AF.Sigmoid), reads=["ps%d" % pb2],
                                 writes=["og%d" % tt])
                if l == 0 and g == 0:
                    dbg("qkT", ar(A_QK, 8), [128, 8, 512], [arn(A_QK + c) for c in range(8)])
                wb, wn = wnext("xq" + tag)
                for c in range(4):
                    pb = c % 2
                    projB(wb, wn, c * 128, 512, 128, pb)
                    copy(evac_eng(), ar(A_XQ + c)[:, 0, :], ps[pb][:], ["ps%d" % pb], [arn(A_XQ + c)])

                if l == 0 and g == 0:
                    dbg("utok", utok[:], [128, 4, 8], ["utok"])
                    dbg("ftok", ftok[:], [128, 4, 8], ["ftok"])
                    dbg("decay_bc", decay_bc[:], [128, 4, 4], ["decay_bc"])
                chk('proj')
                chk('proj%d_%d' % (l, g))
                def swa_S(k):
                    n, hg = k // 2, k % 2
                    qs = slice(n * 128, (n + 1) * 128)
                    b0 = hg * 4
                    for pr in range(2):
                        pb = b0 + pr
                        for hh_ in range(2):
                            h = hg * 4 + pr * 2 + hh_
                            for half in range(2):
                                slot = hh_ * 2 + half
                                ks = slice(n * 128 + half * 128, n * 128 + half * 128 + 128)
                                mm(ps[pb][:, slot * 128:(slot + 1) * 128], kz[:, hg, h % 2, ks],
                                   ar(A_Q + h // 2)[:, 0, qs], True, True, ["kdup", arn(A_Q + h // 2)], ["ps%d" % pb],
                                   mark=(slot == 3))
                        h0 = hg * 4 + pr * 2
                        sc = sc_sb[pr]
                        P.op(DVE, lambda e, pb=pb, sc=sc, h0=h0: e.scalar_tensor_tensor(out=sc[:], in0=ps[pb][:], scalar=0.125,
                                                                                      in1=AP(biasT, h0 * 256, [[1, 512]]), op0=ALU.mult, op1=ALU.add),
                             reads=["ps%d" % pb, "biasT"], writes=["sc_sb%d" % pr])
                        pc = 24 + hg * 2 + pr
                        P.op(ACT, lambda e, sc=sc, pc=pc: e.activation(out=ar(pc)[:, 0, :], in_=sc[:], func=AF.Exp), reads=["sc_sb%d" % pr],
                             writes=[arn(pc)])

                def swa_V(k):
                    n, hg = k // 2, k % 2
                    gb = g * 4 + n
                    po_b = hg * 4 + 2
                    halves = [1] if gb == 0 else [0, 1]
                    for hl in range(4):
                        pr, hh_ = hl // 2, hl % 2
                        pc = 24 + hg * 2 + pr
                        for i_, half in enumerate(halves):
                            slot = hh_ * 2 + half
                            first, last = i_ == 0, i_ == len(halves) - 1
                            mm(ps[po_b][:, hl * 65:(hl + 1) * 65], ar(pc)[:, 0, slot * 128:(slot + 1) * 128], vswa[:, n + half, hg, :],
                               first, last, ["vswa", arn(pc)], ["ps%d" % po_b], mark=(last and hl == 3))
                    P.op(DVE, lambda e, po_b=po_b, hg=hg: e.tensor_tensor(out=AP(swt, hg * 4, [[1, 4], [1, 1]]), in0=AP(ps[po_b], 64, [[65, 4], [1, 1]]),
                                                                         in1=AP(sinkexp, l * 8 + hg * 4, [[1, 4], [1, 1]]), op=ALU.add),
                         reads=["ps%d" % po_b, "sinkexp"], writes=["swt%d" % hg])
                    P.op(DVE, lambda e, hg=hg: e.reciprocal(out=swt[:, hg * 4:hg * 4 + 4], in_=swt[:, hg * 4:hg * 4 + 4]), reads=["swt%d" % hg],
                         writes=["swt%d" % hg])
                    P.op(DVE, lambda e, po_b=po_b, hg=hg: e.tensor_tensor(out=AP(ysw_tok, hg * 256, [[64, 4], [1, 64]]),
                                                                         in0=AP(ps[po_b], 0, [[65, 4], [1, 64]]),
                                                                         in1=AP(swt, hg * 4, [[1, 4], [0, 64]]), op=ALU.mult),
                         reads=["ps%d" % po_b, "swt%d" % hg], writes=["ysw_tok%d" % hg])

                def swa_T(n):
                    tb = 3 if n % 2 == 0 else 7
                    for jc in range(4):
                        mm(ps[tb][:, jc * 128:(jc + 1) * 128], ysw_tok[:, jc * 128:(jc + 1) * 128], ident_bf[:], True, True,
                           ["ysw_tok%d" % (jc // 2), "ident_bf"], ["ps%d" % tb], mark=(jc == 3))
                    copy(ACT, AP(arena, A_YS * 512 + n * 128, [[512, 4], [1, 128]]), AP(ps[tb], 0, [[128, 4], [1, 128]]), ["ps%d" % tb],
                         [arn(A_YS + i_) for i_ in range(4)])

                swa_S(0)
                pend_t = None
                for k in range(8):
                    if k + 1 < 8:
                        swa_S(k + 1)
                    if pend_t is not None:
                        swa_T(pend_t)
                        pend_t = None
                    swa_V(k)
                    if k % 2 == 1:
                        pend_t = k // 2
                swa_T(3)
                copy(POOL, kz[:, :, :, 0:128], kz[:, :, :, 512:640], ["kdup"], ["kdup"])
                copy(POOL, vswa[:, 0, :, :], vswa[:, 4, :, :], ["vswa"], ["vswa"])
                if l == 0 and g == 0:
                    dbg("y_swaT", ar(A_YS, 4), [128, 4, 512], [arn(A_YS + i_) for i_ in range(4)])

                chk('swa')
                def xa_S(hx):
                    b0 = (hx % 2) * 4
                    for half in range(2):
                        pc = 8 + (hx % 2) * 2 + half
                        mm(ps[b0 + half][:], kxT[:, l, hx, half * 128:(half + 1) * 128], ar(A_XQ + hx)[:, 0, :], True, True,
                           ["kxT", arn(A_XQ + hx)], ["ps%d" % (b0 + half)])
                        P.op(ACT, lambda e, b0=b0, half=half, pc=pc: e.activation(out=ar(pc)[:, 0, :], in_=ps[b0 + half][:], func=AF.Exp,
                                                                                scale=128.0 ** -0.5), reads=["ps%d" % (b0 + half)], writes=[arn(pc)])

                def xa_V(hx):
                    b0 = (hx % 2) * 4
                    for tt in range(4):
                        tsl = slice(tt * 128, (tt + 1) * 128)
                        for half in range(2):
                            pc = 8 + (hx % 2) * 2 + half
                            mm(ps[b0 + 2][:, tsl], ar(pc)[:, 0, tsl], vx[:, l, half, hx * 128:(hx + 1) * 128], half == 0, half == 1, ["vx", arn(pc)],
                               ["ps%d" % (b0 + 2)], mark=(half == 1 and tt == 3))
                    for tt in range(4):
                        tsl = slice(tt * 128, (tt + 1) * 128)
                        for half in range(2):
                            pc = 8 + (hx % 2) * 2 + half
                            mm(ps[b0 + 3][:, tt:tt + 1], ar(pc)[:, 0, tsl], ones_bf[:, 0:1], half == 0, half == 1, ["ones_bf", arn(pc)],
                               ["ps%d" % (b0 + 3)], mark=(half == 1 and tt == 3))
                    P.op(DVE, lambda e, b0=b0: e.reciprocal(out=swt[:, 0:4], in_=ps[b0 + 3][:, 0:4]), reads=["ps%d" % (b0 + 3)], writes=["swt0"])
                    P.op(DVE, lambda e, b0=b0: e.tensor_tensor(out=yx_tok[:], in0=AP(ps[b0 + 2], 0, [[128, 4], [1, 128]]),
                                                              in1=AP(swt, 0, [[1, 4], [0, 128]]), op=ALU.mult),
                         reads=["ps%d" % (b0 + 2), "swt0"], writes=["yx_tok"])
                    for tt in range(4):
                        mm(ps[b0 + 3][:, tt * 128:(tt + 1) * 128], yx_tok[:, tt, :], ident_bf[:], True, True, ["yx_tok", "ident_bf"], ["ps%d" % (b0 + 3)],
                           mark=(tt == 3))
                    copy(ACT, ar(A_YX + hx)[:, 0, :], ps[b0 + 3][:], ["ps%d" % (b0 + 3)], [arn(A_YX + hx)])

                xa_S(0)
                for hx in range(4):
                    if hx + 1 < 4:
                        xa_S(hx + 1)
                    xa_V(hx)
                if l == 0 and g == 0:
                    dbg("y_xT", ar(A_YX, 4), [128, 4, 512], [arn(A_YX + i_) for i_ in range(4)])

                chk('xattn')
                for cc in range(4):
                    for par in range(2):
                        Z = 8 + 2 * cc + par
                        P.op(POOL, lambda e, Z=Z, par=par: e.memset(ar(Z, 1, (1 - par) * 64, 64)[:, 0, :], 0.0), writes=[arn(Z)])
                        copy(ACT if par else DVE, ar(Z, 1, par * 64, 64)[:, 0, :], ar(A_QK + 4 + cc, 1, par * 64, 64)[:, 0, :],
                             [arn(A_QK + 4 + cc)], [arn(Z)])
                def ml_front(tt):
                    ts_ = slice(tt * 128, (tt + 1) * 128)
                    for cc in range(4):
                        mm(ps[7][:, cc * 128:(cc + 1) * 128], ar(A_QK + 4 + cc)[:, 0, ts_], ident_bf[:], True, True,
                           [arn(A_QK + 4 + cc), "ident_bf"], ["ps7"], mark=(cc == 3))
                    copy(ACT, ktok[:], ps[7][:], ["ps7"], ["ktok"])
                    for h in range(8):
                        pb = h // 4
                        hl = h % 4
                        mm(ps[pb][:, hl * 128:(hl + 1) * 128], ar(8 + 2 * (h // 2) + h % 2)[:, 0, ts_], ar(A_QK + h // 2)[:, 0, ts_],
                           True, True, [arn(8 + 2 * (h // 2) + h % 2), arn(A_QK + h // 2)], ["ps%d" % pb], mark=(hl == 3))
                    P.op(DVE, lambda e, tt=tt: e.tensor_tensor(out=Cd32[:], in0=Cst[:], in1=AP(decay_bc, tt * 4, [[1, 4], [0, 65]]), op=ALU.mult),
                         reads=["Cst", "decay_bc"], writes=["Cd32"])
                    copy(ACT, AP(Cd16z, 0, [[130, 4], [1, 65]], 0, 64), AP(Cd32, 0, [[65, 4], [1, 65]], 0, 64), ["Cd32"], ["Cd16"])
                    copy(ACT, AP(Cd16z, 65, [[130, 4], [1, 65]], 64, 64), AP(Cd32, 0, [[65, 4], [1, 65]], 64, 64), ["Cd32"], ["Cd16"])
                    for pb in range(2):
                        P.op(DVE, lambda e, pb=pb: e.scalar_tensor_tensor(out=AP(STm, pb * 512, [[128, 4], [1, 128]]),
                                                                         in0=AP(ps[pb], 0, [[128, 4], [1, 128]]), scalar=0.125,
                                                                         in1=AP(mask01, 0, [[0, 4], [1, 128]]), op0=ALU.mult, op1=ALU.mult),
                             reads=["ps%d" % pb, "mask01"], writes=["STm%d" % pb])
                    P.op(POOL, lambda e, tt=tt: e.tensor_tensor(out=vu[:], in0=vtok[:, tt, :, :], in1=AP(utok, tt * 8, [[1, 8], [0, 65]]), op=ALU.mult),
                         reads=["vtok%d" % tt, "utok"], writes=["vu"])

                def ml_mid(tt):
                    ts_ = slice(tt * 128, (tt + 1) * 128)
                    for h in range(8):
                        pb = 2 + h // 4
                        hl = h % 4
                        mm(ps[pb][:, hl * 65:(hl + 1) * 65], STm[:, h, :], vu[:, h, :], True, False, ["STm%d" % (h // 4), "vu"], ["ps%d" % pb], mark=False)
                        mm(ps[pb][:, hl * 65:(hl + 1) * 65], ar(A_QK + h // 2)[:, 0, ts_], Cd16z[:, h, :], False, True,
                           [arn(A_QK + h // 2), "Cd16"], ["ps%d" % pb], mark=(hl == 3))
                    for h in range(8):
                        pb = 4 + h // 4
                        hl = h % 4
                        j = h // 2
                        mm(ps[pb][:, hl * 65:(hl + 1) * 65], ktok[:, j * 128:(j + 1) * 128], vu[:, h, :], True, True, ["ktok", "vu"],
                           ["ps%d" % pb], mark=(hl == 3))

                def ml_state(tt):
                    for pb in range(2):
                        for par in range(2):
                            P.op(DVE, lambda e, pb=pb, par=par: e.scalar_tensor_tensor(
                                out=AP(Cst, pb * 130, [[65, 2], [1, 65]], par * 64, 64),
                                in0=AP(ps[4 + pb], par * 65, [[130, 2], [1, 65]], par * 64, 64), scalar=0.125,
                                in1=AP(Cd32, pb * 130, [[65, 2], [1, 65]], par * 64, 64), op0=ALU.mult, op1=ALU.add),
                                reads=["ps%d" % (4 + pb), "Cd32"], writes=["Cst"])

                def ml_epi(tt):
                    for pb in range(2):
                        P.op(ACT, lambda e, pb=pb: e.activation(out=AP(lnt, pb * 4, [[1, 4], [1, 1]]), in_=AP(ps[2 + pb], 64, [[65, 4], [1, 1]]),
                                                               func=AF.Abs), reads=["ps%d" % (2 + pb)], writes=["lnt0"])
                    P.op(DVE, lambda e, tt=tt: e.tensor_tensor(out=lnt[:, 0, :], in0=lnt[:, 0, :], in1=ftok[:, tt, :], op=ALU.max),
                         reads=["lnt0", "ftok"], writes=["lnt0"])
                    P.op(DVE, lambda e: e.reciprocal(out=lnt[:, 0, :], in_=lnt[:, 0, :]), reads=["lnt0"], writes=["lnt0"])
                    for pb in range(2):
                        P.op(DVE, lambda e, pb=pb: e.tensor_tensor(out=hh[:, pb * 4:pb * 4 + 4, :], in0=AP(ps[2 + pb], 0, [[65, 4], [1, 64]]),
                                                                   in1=AP(lnt, pb * 4, [[1, 4], [0, 64]]), op=ALU.mult),
                             reads=["ps%d" % (2 + pb), "lnt0"], writes=["hh"])
                    if l == 0 and g == 0 and tt == 1:
                        dbg("hh", hh[:], [128, 8, 64], ["hh"])
                    P.op(DVE, lambda e: e.tensor_reduce(out=lnt[:, 1, :], in_=hh[:], axis=AX.X, op=ALU.add), reads=["hh"], writes=["lnt1"])
                    P.op(POOL, lambda e: e.tensor_tensor(out=sqh_ap, in0=hh[:], in1=hh[:], op=ALU.mult), reads=["hh"], writes=["sc_sb0"])
                    P.op(DVE, lambda e: e.tensor_reduce(out=lnt[:, 2, :], in_=sqh_ap, axis=AX.X, op=ALU.add), reads=["sc_sb0"], writes=["lnt2"])
                    P.op(DVE, lambda e: e.tensor_scalar(out=lnt[:, 1, :], in0=lnt[:, 1, :], scalar1=1.0 / 64, scalar2=None, op0=ALU.mult),
                         reads=["lnt1"], writes=["lnt1"])
                    P.op(DVE, lambda e: e.tensor_tensor(out=lnt[:, 3, :], in0=lnt[:, 1, :], in1=lnt[:, 1, :], op=ALU.mult),
                         reads=["lnt1"], writes=["lnt3"])
                    P.op(DVE, lambda e: e.scalar_tensor_tensor(out=lnt[:, 2, :], in0=lnt[:, 2, :], scalar=1.0 / 64, in1=lnt[:, 3, :],
                                                               op0=ALU.mult, op1=ALU.subtract), reads=["lnt2", "lnt3"], writes=["lnt2"])
                    P.op(DVE, lambda e: e.tensor_scalar(out=lnt[:, 2, :], in0=lnt[:, 2, :], scalar1=EPS, scalar2=None, op0=ALU.add),
                         reads=["lnt2"], writes=["lnt2"])
                    P.op(ACT, lambda e: e.activation(out=lnt[:, 2, :], in_=lnt[:, 2, :], func=AF.Sqrt), reads=["lnt2"], writes=["lnt2"])
                    P.op(DVE, lambda e: e.reciprocal(out=lnt[:, 2, :], in_=lnt[:, 2, :]), reads=["lnt2"], writes=["lnt2"])
                    P.op(DVE, lambda e: e.tensor_tensor(out=hh[:], in0=hh[:], in1=AP(lnt, 8, [[1, 8], [0, 64]]), op=ALU.subtract),
                         reads=["hh", "lnt1"], writes=["hh"])
                    P.op(DVE, lambda e: e.tensor_tensor(out=hh[:], in0=hh[:], in1=AP(lnt, 16, [[1, 8], [0, 64]]), op=ALU.mult),
                         reads=["hh", "lnt2"], writes=["hh"])
                    P.op(POOL, lambda e: e.tensor_tensor(out=AP(hh, 0, [[1, 512]]), in0=AP(hh, 0, [[1, 512]]), in1=normg_bc[:], op=ALU.mult),
                         reads=["hh", "normg"], writes=["hh"])
                    P.op(DVE, lambda e, tt=tt: e.tensor_tensor(out=ytok[:], in0=AP(hh, 0, [[1, 512]]), in1=og[:, tt, :], op=ALU.mult),
                         reads=["hh", "og%d" % tt], writes=["ytok"])

                def ml_tr(tt):
                    for jc in range(4):
                        mm(ps[6][:, jc * 128:(jc + 1) * 128], ytok[:, jc * 128:(jc + 1) * 128], ident_bf[:], True, True, ["ytok", "ident_bf"], ["ps6"],
                           mark=(jc == 3))
                    copy(ACT, AP(arena, A_YM * 512 + tt * 128, [[512, 4], [1, 128]]), AP(ps[6], 0, [[128, 4], [1, 128]]), ["ps6"], [arn(A_YM + i_) for i_ in range(4)])

                ml_front(0)
                for tt in range(4):
                    ml_mid(tt)
                    ml_state(tt)
                    if tt + 1 < 4:
                        ml_front(tt + 1)
                    if tt > 0:
                        ml_tr(tt - 1)
                    ml_epi(tt)
                if l == 0 and g == 0:
                    dbg("y_mlT", ar(A_YM, 4), [128, 4, 512], [arn(A_YM + i_) for i_ in range(4)])

                chk('mlstm')
                for j in range(8):
                    wbg, wng = wnext("g%d_" % j + tag)
                    bk = [(j * 6 + i_) % 8 for i_ in range(6)]
                    for b in range(3):
                        for kc in range(8):
                            mm(ps[bk[b]][:], AP(wbg, kc * 384 + b * 128, [[1, 128]]), hT[:, kc, :], kc == 0, kc == 7, [wng] + hres, ["ps%d" % bk[b]])
                        P.op(ACT, lambda e, b=b, bk=bk: e.activation(out=sg[:, b, :], in_=ps[bk[b]][:], func=AF.Sigmoid), reads=["ps%d" % bk[b]],
                             writes=["sg%d" % b])
                    if j == 0:
                        ml_tr(3)
                    wbb, wnb = wnext("br%d_" % j + tag)
                    for c in range(4):
                        mm(ps[bk[3]][:], AP(wbb, c * 128, [[1, 128]]), ar(A_YS + c)[:, 0, :], c == 0, c == 3, [wnb, arn(A_YS + c)],
                           ["ps%d" % bk[3]])
                    for c in range(4):
                        mm(ps[bk[4]][:], AP(wbb, 512 + c * 128, [[1, 128]]), ar(A_YM + c)[:, 0, :], c == 0, c == 3, [wnb, arn(A_YM + c)], ["ps%d" % bk[4]])
                    for c in range(4):
                        mm(ps[bk[5]][:], AP(wbb, 1024 + c * 128, [[1, 128]]), ar(A_YX + c)[:, 0, :], c == 0, c == 3, [wnb, arn(A_YX + c)], ["ps%d" % bk[5]])
                    P.op(DVE, lambda e, bk=bk: e.tensor_tensor(out=mtmp[0][:], in0=ps[bk[3]][:], in1=sg[:, 0, :], op=ALU.mult),
                         reads=["ps%d" % bk[3], "sg0"], writes=["mtmp0"])
                    P.op(DVE, lambda e, bk=bk: e.tensor_tensor(out=mtmp[1][:], in0=ps[bk[4]][:], in1=sg[:, 1, :], op=ALU.mult),
                         reads=["ps%d" % bk[4], "sg1"], writes=["mtmp1"])
                    P.op(DVE, lambda e: e.tensor_tensor(out=mtmp[0][:], in0=mtmp[0][:], in1=mtmp[1][:], op=ALU.add),
                         reads=["mtmp0", "mtmp1"], writes=["mtmp0"])
                    P.op(DVE, lambda e, bk=bk: e.tensor_tensor(out=mtmp[1][:], in0=ps[bk[5]][:], in1=sg[:, 2, :], op=ALU.mult),
                         reads=["ps%d" % bk[5], "sg2"], writes=["mtmp1"])
                    P.op(DVE, lambda e, j=j: e.tensor_tensor(out=ar(A_MG + j)[:, 0, :], in0=mtmp[0][:], in1=mtmp[1][:], op=ALU.add),
                         reads=["mtmp0", "mtmp1"], writes=[arn(A_MG + j)])
                if l == 0 and g == 0:
                    dbg("mergedT", ar(A_MG, 8), [128, 8, 512], [arn(A_MG + i_) for i_ in range(8)])
                chk('merge')
                for i in range(2):
                    wb, wn = wnext("wo%d_" % i + tag)
                    for dcl in range(4):
                        dc = i * 4 + dcl
                        pb = dc % 4
                        for kc in range(8):
                            mm(ps[pb][:], AP(wb, kc * 512 + dcl * 128, [[1, 128]]), ar(A_MG + kc)[:, 0, :], kc == 0, kc == 7, [wn, arn(A_MG + kc)],
                               ["ps%d" % pb])
                        P.op(DVE, lambda e, dc=dc, pb=pb: e.tensor_tensor(out=xT[:, dc, t0:t0 + TG], in0=ps[pb][:], in1=xT[:, dc, t0:t0 + TG], op=ALU.add),
                             reads=["ps%d" % pb, xres(dc, g)], writes=[xres(dc, g)])
                        if dc >= 2:
                            norm_sq(dc - 2, xT[:, dc - 2, t0:t0 + TG], xres(dc - 2, g), 4, TG)
                for dc in (6, 7):
                    norm_sq(dc, xT[:, dc, t0:t0 + TG], xres(dc, g), 4, TG)
                chk('outproj')
                norm_fin([xT[:, c, t0:t0 + TG] for c in range(8)], [xres(c, g) for c in range(8)], 2 + l, [hTb[:, c, :] for c in range(8)],
                         ["hTb%d" % c for c in range(8)], TG, 4)
                hbres = ["hTb%d" % c for c in range(8)]
                for i in range(8):
                    wb, wn = wnext("f1%d_" % i + tag)
                    if i == 0:
                        for kc in range(8):
                            for fl in range(4):
                                mm(ps[fl][:], AP(wb, kc * 512 + fl * 128, [[1, 128]]), hTb[:, kc, :], kc == 0, kc == 7, [wn, "hTb%d" % kc], ["ps%d" % fl])
                    for fl in range(4):
                        f = i * 4 + fl
                        pb = f % 4
                        for kc in range(8):
                            if i == 0:
                                break
                            mm(ps[pb][:], AP(wb, kc * 512 + fl * 128, [[1, 128]]), hTb[:, kc, :], kc == 0, kc == 7, [wn] + hbres, ["ps%d" % pb])
                        rb = r32[f % 2]
                        P.op(ACT, lambda e, pb=pb, rb=rb: e.activation(out=rb[:], in_=ps[pb][:], func=AF.Relu), reads=["ps%d" % pb], writes=["mtmp%d" % (f % 2)])
                        eng = DVE if f % 2 == 0 else POOL
                        P.op(eng, lambda e, rb=rb, f=f: e.tensor_tensor(out=ar(f)[:, 0, :], in0=rb[:], in1=rb[:], op=ALU.mult), reads=["mtmp%d" % (f % 2)],
                             writes=[arn(f)])
                for dc in range(8):
                    wb, wn = wnext("f2%d_" % dc + tag)
                    pb = 4 + dc % 4
                    for f in range(32):
                        mm(ps[pb][:], AP(wb, f * 128, [[1, 128]]), ar(f)[:, 0, :], f == 0, f == 31, [wn, arn(f)], ["ps%d" % pb])
                    P.op(DVE, lambda e, dc=dc, pb=pb: e.tensor_tensor(out=xT[:, dc, t0:t0 + TG], in0=ps[pb][:], in1=xT[:, dc, t0:t0 + TG], op=ALU.add),
                         reads=["ps%d" % pb, xres(dc, g)], writes=[xres(dc, g)])
                    if dc == 0:
                        if g + 1 < NTG:
                            mixer_norm(l, g + 1, 0)
                        elif l + 1 < DEPTH:
                            mixer_norm(l + 1, 0, 0)
                chk('tgend%d_%d' % (l, g))
                if l == 0 and g == 0:
                    dbg("xT0", xT[:, :, 0:TG], [128, 8, TG], [xres(c, 0) for c in range(8)])

        P.barrier()
        dma(SP, gfin_bc[:], wAP(gfin_d, 0, [[0, 128], [1, 1024]]), ["gfin_bc"])
        for i in range(T // 128):
            g = i // 4
            xf = xs_f[i % 2]
            xfn = "xs_f%d" % (i % 2)
            for half in range(2):
                pb = 1 + half + 2 * (i % 2)
                for j in range(4):
                    c = half * 4 + j
                    P.op(PE, lambda e, pb=pb, j=j, c=c, i=i: e.transpose(ps[pb][:, j * 128:(j + 1) * 128], xT[:, c, i * 128:(i + 1) * 128], ident[:]),
                         reads=[xres(c, g), "ident"], writes=["ps%d" % pb], mark=(j == 3))
                copy(evac_eng(), xf[:, half * 512:(half + 1) * 512], ps[pb][:], ["ps%d" % pb], [xfn])
            P.op(ACT, lambda e, xf=xf, i=i: e.activation(out=fsq[:], in_=xf[:], func=AF.Square, accum_out=fss[:, i:i + 1]), reads=[xfn],
                 writes=["fsq", "fss%d" % i])
            P.op(ACT, lambda e, i=i: e.activation(out=fss[:, i:i + 1], in_=fss[:, i:i + 1], func=AF.Ln, scale=1.0 / D, bias=epscol[:, 0:1]),
                 reads=["fss%d" % i, "epscol"], writes=["fss%d" % i])
            P.op(ACT, lambda e, i=i: e.activation(out=fss[:, i:i + 1], in_=fss[:, i:i + 1], func=AF.Exp, scale=-0.5), reads=["fss%d" % i],
                 writes=["fss%d" % i])
            P.op(DVE, lambda e, xf=xf, i=i: e.scalar_tensor_tensor(out=xf[:], in0=xf[:], scalar=fss[:, i:i + 1], in1=gfin_bc[:], op0=ALU.mult, op1=ALU.mult),
                 reads=[xfn, "fss%d" % i, "gfin_bc"], writes=[xfn])
            out_toks.append(dma(SP, out_d.ap()[i * 128:(i + 1) * 128, :], xf[:], ["out%d" % i], reads=[xfn]))
    except _Stop:
        stopped = True
    for name in dbg_out:
        out_toks.append(P.res["dbg_" + name][0])
    P.final_wait(SP, out_toks)
    assert stopped or wstate["used"] == len(wq), (wstate, len(wq))
    P.emit()
    return nc, dbg_out


_CACHE = {}


def kernel(**inputs):
    if "nc" not in _CACHE:
        _CACHE["nc"] = build()[0]
    nc = _CACHE["nc"]
    consts = host_consts()
    names = ["rel_bias", "g_mix", "w_in", "conv_w", "b_i", "b_f", "mlstm_norm_g", "sinks", "g_mem", "w_mem_kv", "w_br_swa",
             "w_br_mlstm", "w_br_x", "w_out", "g_ffn", "w_ff1", "w_ff2", "g_final"]
    shared = {k: np.ascontiguousarray(np.asarray(inputs[k], dtype=np.float32)) for k in names}
    x = np.asarray(inputs["x"], dtype=np.float32)
    mem = np.asarray(inputs["mem"], dtype=np.float32)
    in_maps = []
    for b in range(8):
        m = dict(shared)
        m.update(consts)
        m["x"] = np.ascontiguousarray(x[b])
        m["mem"] = np.ascontiguousarray(mem[b])
        in_maps.append(m)
    res = run_bass_kernel_spmd(nc, in_maps, core_ids=list(range(8)))
    return np.stack([np.asarray(res.results[b]["out"], dtype=np.float32) for b in range(8)], axis=0)
```
